# Optimizing a Trainium2 kernel written in Bass

```python
import jax, jax.numpy as jnp
from jax import lax
import numpy as np

D_MODEL = 1024
BATCH = 32
SEQ = 2048
DEPTH = 2

GRID_W = 64
ATTN_HEADS = 16
ATTN_KV_HEADS = 4
ATTN_HEAD_DIM = 64
ATTN_GROUP = ATTN_HEADS // ATTN_KV_HEADS
ATTN_WIDTH = ATTN_HEADS * ATTN_HEAD_DIM
ATTN_KV_WIDTH = ATTN_KV_HEADS * ATTN_HEAD_DIM
ROPE_AXIS_DIM = ATTN_HEAD_DIM // 2
ROPE_THETA = 10000.0
Q_BLOCK = 128
HGRN_EXPAND = 128
HGRN_HEADS = D_MODEL // HGRN_EXPAND
HGRN_HEAD_V = 128
HGRN_KEY_WIDTH = HGRN_HEADS * HGRN_EXPAND
HGRN_VAL_WIDTH = HGRN_HEADS * HGRN_HEAD_V
HGRN_CHUNK = 32
N_DIRECTIONS = 2
NORM_EPS = 1e-6

SPLIT_SIZES = (ATTN_WIDTH, ATTN_KV_WIDTH, ATTN_KV_WIDTH, ATTN_WIDTH,
               HGRN_KEY_WIDTH, HGRN_KEY_WIDTH, HGRN_KEY_WIDTH, HGRN_VAL_WIDTH, HGRN_VAL_WIDTH,
               D_MODEL, D_MODEL)
N_IN = sum(SPLIT_SIZES)

kernel_name = "hybrid_gqa_axialrope_hgrn2_bidir"


def _split_indices():
    return [int(i) for i in np.cumsum(SPLIT_SIZES)[:-1]]


def _rms_norm(x, w):
    xf = x.astype(jnp.float32)
    y = xf * lax.rsqrt(jnp.mean(xf * xf, axis=-1, keepdims=True) + NORM_EPS)
    return (y * w.astype(jnp.float32)).astype(x.dtype)


def _axial_rope(seq_len):
    rows = seq_len // GRID_W
    row = jnp.repeat(jnp.arange(rows), GRID_W).astype(jnp.float32)
    col = jnp.tile(jnp.arange(GRID_W), rows).astype(jnp.float32)
    inv_freq = ROPE_THETA ** (-jnp.arange(0, ROPE_AXIS_DIM, 2, dtype=jnp.float32) / ROPE_AXIS_DIM)
    ang_r = row[:, None] * inv_freq
    ang_c = col[:, None] * inv_freq
    return jnp.cos(ang_r), jnp.sin(ang_r), jnp.cos(ang_c), jnp.sin(ang_c)


def _rotate(seg, c, s):
    half = seg.shape[-1] // 2
    x1, x2 = seg[..., :half], seg[..., half:]
    return jnp.concatenate([x1 * c - x2 * s, x2 * c + x1 * s], axis=-1)


def _apply_axial_rope(x, rope):
    cr, sr, cc, sc = [t[None, :, None, :] for t in rope]
    xf = x.astype(jnp.float32)
    out = jnp.concatenate([_rotate(xf[..., :ROPE_AXIS_DIM], cr, sr),
                           _rotate(xf[..., ROPE_AXIS_DIM:], cc, sc)], axis=-1)
    return out.astype(x.dtype)


def _blocked_attention(q, k, v):
    B, S, _, hd = q.shape
    nb = S // Q_BLOCK
    qb = q.reshape(B, nb, Q_BLOCK, ATTN_KV_HEADS, ATTN_GROUP, hd).transpose(1, 0, 3, 4, 2, 5)
    kt = k.transpose(0, 2, 1, 3)
    vt = v.transpose(0, 2, 1, 3)
    scale = hd ** -0.5

    def one_block(q_blk):
        s = jnp.einsum('bkgqd,bksd->bkgqs', q_blk, kt).astype(jnp.float32) * scale
        p = jax.nn.softmax(s, axis=-1).astype(vt.dtype)
        return jnp.einsum('bkgqs,bksd->bkgqd', p, vt)

    o = lax.map(one_block, qb)
    return o.transpose(1, 0, 4, 2, 3, 5).reshape(B, S, ATTN_WIDTH)


def _attention_branch(aq, ak, av, q_norm_w, k_norm_w, rope):
    B, S, _ = aq.shape
    q = aq.reshape(B, S, ATTN_HEADS, ATTN_HEAD_DIM)
    k = ak.reshape(B, S, ATTN_KV_HEADS, ATTN_HEAD_DIM)
    v = av.reshape(B, S, ATTN_KV_HEADS, ATTN_HEAD_DIM)
    q = _apply_axial_rope(_rms_norm(q, q_norm_w), rope)
    k = _apply_axial_rope(_rms_norm(k, k_norm_w), rope)
    return _blocked_attention(q, k, v)


def _hgrn2_chunk_scan(q, k, v, log_f):
    B, H, S, DK = q.shape
    DV = v.shape[-1]
    n = S // HGRN_CHUNK

    def to_chunks(a):
        return a.reshape(B, H, n, HGRN_CHUNK, a.shape[-1]).transpose(2, 0, 1, 3, 4)

    mask = jnp.tril(jnp.ones((HGRN_CHUNK, HGRN_CHUNK), dtype=bool))[None, None, :, :, None]

    def step(state, inp):
        q_c, k_c, v_c, g_c = inp
        b = jnp.cumsum(g_c, axis=2)
        diff = b[:, :, :, None, :] - b[:, :, None, :, :]
        decay = jnp.exp(jnp.where(mask, diff, -jnp.inf))
        scores = jnp.einsum('bhtk,bhsk,bhtsk->bhts', q_c, k_c, decay)
        o = jnp.einsum('bhts,bhsv->bhtv', scores, v_c) \
            + jnp.einsum('bhtk,bhkv->bhtv', q_c * jnp.exp(b), state)
        b_last = b[:, :, -1:, :]
        state = jnp.exp(b_last[:, :, 0, :])[..., None] * state \
            + jnp.einsum('bhsk,bhsv->bhkv', k_c * jnp.exp(b_last - b), v_c)
        return state, o

    state0 = jnp.zeros((B, H, DK, DV), jnp.float32)
    _, o = lax.scan(step, state0, (to_chunks(q), to_chunks(k), to_chunks(v), to_chunks(log_f)))
    return o.transpose(1, 2, 0, 3, 4).reshape(B, H, S, DV)


def _hgrn2_branch(hq, hf_fwd, hf_bwd, hi, lb_layer, g_norm_w):
    B, S, _ = hq.shape

    def heads(a, d):
        return a.reshape(B, S, HGRN_HEADS, d).transpose(0, 2, 1, 3).astype(jnp.float32)

    q = jax.nn.silu(heads(hq, HGRN_EXPAND)) * (HGRN_EXPAND ** -0.5)
    v = heads(hi, HGRN_HEAD_V)

    def direction(z_raw, lb, backward):
        z = heads(z_raw, HGRN_EXPAND)
        lb = lb.reshape(1, HGRN_HEADS, 1, HGRN_EXPAND)
        log_f = jnp.logaddexp(jnp.log(lb), jnp.log1p(-lb) + jax.nn.log_sigmoid(z))
        k = -jnp.expm1(log_f)
        if backward:
            o = _hgrn2_chunk_scan(q[:, :, ::-1], k[:, :, ::-1], v[:, :, ::-1], log_f[:, :, ::-1])
            return o[:, :, ::-1]
        return _hgrn2_chunk_scan(q, k, v, log_f)

    o = direction(hf_fwd, lb_layer[0], False) + direction(hf_bwd, lb_layer[1], True)
    o = _rms_norm(o, g_norm_w)
    return o.transpose(0, 2, 1, 3).reshape(B, S, HGRN_VAL_WIDTH).astype(hq.dtype)


def setup_inputs(seed: int = 0) -> dict:
    key = jax.random.key(seed)
    ks = jax.random.split(key, 12)
    f32 = jnp.float32
    x = jax.random.normal(ks[0], (BATCH, SEQ, D_MODEL), f32)
    w_in = jax.random.normal(ks[1], (DEPTH, D_MODEL, N_IN), f32) * D_MODEL ** -0.5
    norm_w = 1.0 + 0.02 * jax.random.normal(ks[2], (DEPTH, D_MODEL), f32)
    q_norm_w = 1.0 + 0.02 * jax.random.normal(ks[3], (DEPTH, ATTN_HEAD_DIM), f32)
    k_norm_w = 1.0 + 0.02 * jax.random.normal(ks[4], (DEPTH, ATTN_HEAD_DIM), f32)
    hgrn_lower_bounds = 0.1 * jax.random.normal(ks[5], (N_DIRECTIONS, DEPTH, HGRN_KEY_WIDTH), f32)
    hgrn_norm_w = 1.0 + 0.02 * jax.random.normal(ks[6], (DEPTH, HGRN_HEAD_V), f32)
    w_branch_attn = jax.random.normal(ks[7], (DEPTH, ATTN_WIDTH, D_MODEL), f32) * ATTN_WIDTH ** -0.5
    w_branch_hgrn = jax.random.normal(ks[8], (DEPTH, HGRN_VAL_WIDTH, D_MODEL), f32) * HGRN_VAL_WIDTH ** -0.5
    w_out = jax.random.normal(ks[9], (DEPTH, D_MODEL, D_MODEL), f32) * D_MODEL ** -0.5
    final_norm_w = 1.0 + 0.02 * jax.random.normal(ks[10], (D_MODEL,), f32)
    return {"x": x, "w_in": w_in, "norm_w": norm_w, "q_norm_w": q_norm_w, "k_norm_w": k_norm_w,
            "hgrn_lower_bounds": hgrn_lower_bounds, "hgrn_norm_w": hgrn_norm_w,
            "w_branch_attn": w_branch_attn, "w_branch_hgrn": w_branch_hgrn, "w_out": w_out,
            "final_norm_w": final_norm_w}


def reference(x, w_in, norm_w, q_norm_w, k_norm_w, hgrn_lower_bounds, hgrn_norm_w,
              w_branch_attn, w_branch_hgrn, w_out, final_norm_w):
    S = x.shape[1]
    rope = _axial_rope(S)
    lb = jnp.cumsum(jax.nn.softmax(hgrn_lower_bounds.astype(jnp.float32), axis=1), axis=1)
    lb = lb - lb[:, :1]
    split_at = _split_indices()
    for layer in range(DEPTH):
        h = _rms_norm(x, norm_w[layer])
        proj = jnp.einsum('bsd,de->bse', h, w_in[layer])
        (aq, ak, av, a_gate, hq, hf_fwd, hf_bwd, hi, h_gate, m_attn, m_hgrn) = jnp.split(proj, split_at, axis=-1)
        y_attn = _attention_branch(aq, ak, av, q_norm_w[layer], k_norm_w[layer], rope) * jax.nn.silu(a_gate)
        y_hgrn = _hgrn2_branch(hq, hf_fwd, hf_bwd, hi, lb[:, layer], hgrn_norm_w[layer]) * jax.nn.silu(h_gate)
        merged = jax.nn.sigmoid(m_attn) * jnp.einsum('bse,ed->bsd', y_attn, w_branch_attn[layer]) \
            + jax.nn.sigmoid(m_hgrn) * jnp.einsum('bse,ed->bsd', y_hgrn, w_branch_hgrn[layer])
        x = x + jnp.einsum('bsd,de->bse', merged, w_out[layer])
    return _rms_norm(x, final_norm_w)
```

```python
import os
import numpy as np
from contextlib import ExitStack
import concourse.bass as bass
import concourse.mybir as mybir
from concourse.bass_utils import run_bass_kernel_spmd
from concourse.ap import AP

F32 = mybir.dt.float32
BF16 = mybir.dt.bfloat16
AF = mybir.ActivationFunctionType
ALU = mybir.AluOpType
AX = mybir.AxisListType

S = 2048
D = 1024
NT = 16
NBLK = 100
EPS = 1e-6
CH = 4096
KD = 16

O_AQ, O_AK, O_AV, O_AG, O_HQ, O_HFF, O_HFB, O_HI, O_HG, O_MA, O_MH = (
    0, 1024, 1280, 1536, 2560, 3584, 4608, 5632, 6656, 7680, 8704)


class Buf:
    __slots__ = ("w", "r", "name")

    def __init__(self, name=""):
        self.w = None
        self.r = []
        self.name = name


class Op:
    __slots__ = ("stream", "dom", "fn", "deps", "sig", "idx", "isdma")


class Sched:
    STREAMS = ("pe", "act", "dve", "pool", "sp")

    def __init__(self):
        self.ops = {s: [] for s in self.STREAMS}
        self.nsig = {}
        self.ndma = {}

    def op(self, stream, fn, reads=(), writes=(), dma=False, ss=False):
        o = Op()
        o.stream = stream
        o.isdma = dma
        o.dom = ("dma_" + stream) if dma else stream
        o.fn = fn
        o.sig = 0
        deps = set()
        for b in reads:
            if b.w is not None:
                deps.add(b.w)
        for b in writes:
            if b.w is not None:
                deps.add(b.w)
            deps.update(b.r)
        for b in writes:
            b.w = o
            b.r = []
        for b in reads:
            if b.w is not o:
                b.r.append(o)
        deps.discard(o)
        need = []
        for d in deps:
            if d.isdma or dma or d.stream != stream or ss:
                need.append(d)
        o.deps = need
        if dma:
            n = self.ndma.get(o.dom, 0) + 1
            self.ndma[o.dom] = n
            o.idx = n
        self.ops[stream].append(o)
        return o

    def finalize(self):
        needed = set()
        for s in self.STREAMS:
            for o in self.ops[s]:
                for d in o.deps:
                    if not d.isdma:
                        needed.add(d)
        for s in self.STREAMS:
            k = 0
            for o in self.ops[s]:
                if (not o.isdma) and o in needed:
                    k += 1
                    o.sig = k
            self.nsig[s] = k

    def emit(self, stream, eng, sems, final_wait=False):
        waited_c = {}
        waited_d = {}
        for o in self.ops[stream]:
            wc = {}
            wd = {}
            for d in o.deps:
                if d.isdma:
                    si = (d.idx - 1) % KD
                    val = 16 * ((d.idx - 1) // KD + 1)
                    key = (d.dom, si)
                    if wd.get(key, 0) < val:
                        wd[key] = val
                else:
                    if wc.get(d.dom, 0) < d.sig:
                        wc[d.dom] = d.sig
            if o.isdma and o.idx > KD:
                key = (o.dom, (o.idx - 1) % KD)
                val = 16 * ((o.idx - 1) // KD)
                if wd.get(key, 0) < val:
                    wd[key] = val
            for dom, k in wc.items():
                if waited_c.get(dom, 0) >= k:
                    continue
                waited_c[dom] = k
                eng.wait_ge(sems[dom][(k - 1) // CH], (k - 1) % CH + 1)
            for key, val in wd.items():
                if waited_d.get(key, 0) >= val:
                    continue
                waited_d[key] = val
                eng.wait_ge(sems[key[0]][key[1]], val)
            ins = o.fn(eng)
            if o.isdma:
                ins.then_inc(sems[o.dom][(o.idx - 1) % KD], 16)
            elif o.sig:
                ins.then_inc(sems[o.dom][(o.sig - 1) // CH], 1)
        if final_wait:
            for dom, n in self.ndma.items():
                for si in range(min(KD, n)):
                    cnt = (n - 1 - si) // KD + 1
                    eng.wait_ge(sems[dom][si], 16 * cnt)


def V(t, off, dims, p0=0, npart=128):
    row = 1
    for d in t.shape[1:]:
        row *= d
    return AP(t, p0 * row + off, [[row, npart]] + [[int(s), int(c)] for s, c in dims])


def build(nseq, nl):
    nc = bass.Bass("TRN2", target_bir_lowering=False)
    T = nseq * S
    sc = Sched()
    es = ExitStack()

    def dram(name, shape, dt, kind="Internal"):
        return nc.dram_tensor(name, list(shape), dt, kind=kind)

    x_d = dram("x", [T, D], F32, "ExternalInput")
    wall_d = dram("wall", [2, NBLK, 128, 1024], F32, "ExternalInput")
    cm_d = dram("cmat", [6, 128, 128], F32, "ExternalInput")
    smask_d = dram("smask", [128, S], F32, "ExternalInput")
    rope_d = dram("rope", [2, S, 64], F32, "ExternalInput")
    nw_d = dram("nw", [2, D], F32, "ExternalInput")
    fnw_d = dram("fnw", [1, D], F32, "ExternalInput")
    qkw_d = dram("qkw", [2, 320], F32, "ExternalInput")
    lb_d = dram("lbraw", [128, 32], F32, "ExternalInput")
    gw_d = dram("gw", [128, 2], F32, "ExternalInput")
    out_d = dram("out", [T, D], F32, "ExternalOutput")
    wsc_d = dram("wsc", [2, NBLK, 128, 1024], BF16)
    x1_d = dram("x1s", [T, D], F32)
    ya_d = dram("yas", [8, 128, S], BF16)
    DBG = os.environ.get("MK_DBG", "0") == "1"
    DBGM = int(os.environ.get("MK_DBGM", "15"))
    if DBG:
        d_hT = dram("d_hT", [128, 8 * S], BF16, "ExternalOutput")
        d_ya = dram("d_ya", [128, 8 * S], BF16, "ExternalOutput")
        d_yh = dram("d_yh", [128, 8 * S], BF16, "ExternalOutput")
        d_x1 = dram("d_x1", [S, D], F32, "ExternalOutput")

    def sb(name, shape, dt):
        return es.enter_context(nc.sbuf_tensor(name, list(shape), dt))

    def ps(name, shape, dt):
        return es.enter_context(nc.psum_tensor(name, list(shape), dt))

    ident = sb("ident", [128, 128], BF16)
    cmat = sb("cmatsb", [128, 5, 128], F32)
    smask = sb("smasksb", [128, S], BF16)
    ropeC = sb("ropeC", [128, NT, 64], F32)
    ropeS = sb("ropeS", [128, NT, 64], F32)
    nwb = sb("nwb", [128, D], F32)
    wqkb = sb("wqkb", [128, 2, 320], F32)
    lbraw = sb("lbrawsb", [128, 32], F32)
    lbt = sb("lbt", [128, 32], F32)
    omlt = sb("omlt", [128, 32], F32)
    gw = sb("gwsb", [128, 2], F32)
    epsb = sb("epsb", [128, 1], F32)
    lncb = sb("lncb", [128, 1], F32)
    small = sb("small", [128, 64], F32)
    hT = sb("hT", [128, 8, S], BF16)
    yaT = sb("yaT", [128, 8, S], BF16)
    yhT = sb("yhT", [128, 8, S], BF16)
    Va = sb("Va", [128, NT, 2, 128], BF16)
    wring = sb("wring", [128, 2, 5, 1024], BF16)
    ARENA_BYTES = 63232
    arena = sb("arena", [128, ARENA_BYTES // 4], F32)
    pbank = [ps("pb%d" % i, [128, 512], F32) for i in range(7)]
    pT = ps("pbT", [128, 1024], BF16)

    B_const = Buf("const")
    B_small = {}

    arena_bf = arena.bitcast(BF16)
    ya_f32 = yaT.bitcast(F32)
    ARENAS = [(arena, arena_bf, ARENA_BYTES), (ya_f32, yaT, 8 * S * 2)]

    class Region:
        def __init__(self, off_bytes, n, dt, which=0):
            self.dt = dt
            self.n = n
            if dt == F32:
                self.t = ARENAS[which][0]
                self.off = off_bytes // 4
            else:
                self.t = ARENAS[which][1]
                self.off = off_bytes // 2

        def ap(self, off=0, dims=None, p0=0, npart=128):
            if dims is None:
                dims = [(1, self.n - off)]
            return V(self.t, self.off + off, dims, p0, npart)

    class Carver:
        def __init__(self, which=0):
            self.pos = 0
            self.which = which

        def take(self, n, dt):
            sz = n * (4 if dt == F32 else 2)
            r = Region(self.pos, n, dt, self.which)
            self.pos += (sz + 63) // 64 * 64
            assert self.pos <= ARENAS[self.which][2], ("arena overflow", self.which, self.pos)
            return r

    arena_bufs_prev = []

    def inherit(new_bufs, old_bufs):
        olds = []
        for b in old_bufs:
            if b.w is not None:
                olds.append(b.w)
            olds.extend(b.r)
        for nb in new_bufs:
            nb.r = list(olds)

    def mm(out, lhsT, rhs, start, stop, reads, writes):
        return sc.op("pe", lambda e: e.matmul(out, lhsT=lhsT, rhs=rhs, start=start, stop=stop), reads, writes)

    def tr(out, in_, reads, writes):
        return sc.op("pe", lambda e: e.transpose(out, in_, ident[:, :]), list(reads) + [B_const], writes)

    def act(out, in_, func, reads, writes, scale=1.0, bias=None, accum_out=None, ss=False):
        def f(e):
            kw = {}
            if bias is not None:
                kw["bias"] = bias
            if accum_out is not None:
                kw["accum_out"] = accum_out
            return e.activation(out=out, in_=in_, func=func, scale=scale, **kw)
        return sc.op("act", f, reads, writes, ss=ss)

    def tt(eng, out, in0, in1, op, reads, writes, ss=False):
        return sc.op(eng, lambda e: e.tensor_tensor(out=out, in0=in0, in1=in1, op=op), reads, writes, ss=ss)

    def ts(eng, out, in0, s1, s2, op0, op1, reads, writes):
        if op1 is None:
            return sc.op(eng, lambda e: e.tensor_scalar(out=out, in0=in0, scalar1=s1, scalar2=None, op0=op0), reads, writes)
        return sc.op(eng, lambda e: e.tensor_scalar(out=out, in0=in0, scalar1=s1, scalar2=s2, op0=op0, op1=op1), reads, writes)

    def stt(out, in0, scalar, in1, op0, op1, reads, writes, ss=False):
        return sc.op("dve", lambda e: e.scalar_tensor_tensor(out=out, in0=in0, scalar=scalar, in1=in1, op0=op0, op1=op1), reads, writes, ss=ss)

    def cp(eng, out, in_, reads, writes):
        if eng == "act":
            return act(out, in_, AF.Copy, reads, writes)
        return sc.op(eng, lambda e: e.tensor_copy(out=out, in_=in_), reads, writes)

    def recip(out, in_, reads, writes, ss=False):
        return sc.op("dve", lambda e: e.reciprocal(out=out, in_=in_), reads, writes, ss=ss)

    def dma(q, out, in_, reads, writes):
        return sc.op(q, lambda e: e.dma_start(out=out, in_=in_), reads, writes, dma=True)

    NPAIR = NBLK // 2
    B_wsc = [[Buf("wsc%d_%d" % (l, p)) for p in range(NPAIR)] for l in range(2)]
    for l in range(nl):
        for p in range(NPAIR):
            dma("pool", wsc_d[l, 2 * p:2 * p + 2].rearrange("a p c -> (a p) c"),
                wall_d[l, 2 * p:2 * p + 2].rearrange("a p c -> (a p) c"), [], [B_wsc[l][p]])
    dma("pool", ident[:, :], cm_d[0], [], [B_const])
    dma("pool", smask[:, :], smask_d[:, :], [], [B_const])
    dma("sp", cmat[:, :, :], cm_d[1:6].rearrange("a p c -> p a c"), [], [B_const])
    dma("sp", ropeC[:, :, :], rope_d[0].rearrange("(i p) c -> p i c", p=128), [], [B_const])
    dma("sp", ropeS[:, :, :], rope_d[1].rearrange("(i p) c -> p i c", p=128), [], [B_const])
    dma("sp", V(wqkb, 0, [(1, 640)]), AP(qkw_d, 0, [[0, 128], [1, 640]]), [], [B_const])
    dma("sp", lbraw[:, :], lb_d[:, :], [], [B_const])
    dma("sp", gw[:, :], gw_d[:, :], [], [B_const])
    sc.op("pool", lambda e: e.memset(Va[:, :, :, :], 1.0), [], [B_const])
    sc.op("dve", lambda e: e.memset(epsb[:, :], EPS), [], [B_const])
    sc.op("dve", lambda e: e.memset(lncb[:, :], float(np.log(128.0 ** -0.5))), [], [B_const])
    sc.op("dve", lambda e: e.memset(lbt[:, :], 0.0), [], [B_const])
    tt("dve", lbt[:, 16:32], lbraw[:, 16:32], lbraw[:, 0:16], ALU.subtract, [B_const], [B_const])
    act(lbt[:, 16:32], lbt[:, 16:32], AF.Sigmoid, [B_const], [B_const])
    ts("dve", omlt[:, :], lbt[:, :], -1.0, 1.0, ALU.mult, ALU.add, [B_const], [B_const])

    maskF = cmat[:, 0, :]
    maskB = cmat[:, 1, :]
    selsw = cmat[:, 2, :]
    onesf = cmat[:, 3, :]
    rowmask = cmat[:, 4, :]

    B_hT = [Buf("hT%d" % i) for i in range(NT)]
    B_ya = [[Buf("ya%d_%d" % (f, tb)) for tb in range(4)] for f in range(8)]
    B_yh = [[Buf("yh%d_%d" % (j, tb)) for tb in range(4)] for j in range(8)]
    B_pb = [Buf("pb%d" % i) for i in range(7)]
    B_pT = Buf("pT")
    B_wr = [Buf("wr0"), Buf("wr1")]
    B_Va = [Buf("Va%d" % i) for i in range(NT)]
    B_nwb = Buf("nwb")
    B_yad = [Buf("yad%d" % i) for i in range(8)]
    B_x1 = [Buf("x1_%d" % i) for i in range(NT)]
    wr_state = {"n": 0}

    def load_w(l, b0, nb):
        sl = wr_state["n"] % 2
        wr_state["n"] += 1
        reads = [B_wsc[l][p] for p in range(b0 // 2, (b0 + nb - 1) // 2 + 1)]
        dma("sp", V(wring, sl * 5 * 1024, [(1024, nb), (1, 1024)]),
            wsc_d[l, b0:b0 + nb].rearrange("e p c -> p e c"), reads, [B_wr[sl]])
        return sl, B_wr[sl]

    def wblk(sl, b, kt, c0=0, ncol=128):
        return V(wring, sl * 5 * 1024 + b * 1024 + kt * 128 + c0, [(1, ncol)])

    def wrow(sl, b0, nb, kt):
        return V(wring, sl * 5 * 1024 + b0 * 1024 + kt * 128, [(1024, nb), (1, 128)])

    def hTs(kt, t0, n):
        return V(hT, kt * S + t0, [(1, n)])

    pbrr = {"n": 0}

    def next_pb(lo=0, hi=2):
        i = lo + pbrr["n"] % (hi - lo)
        pbrr["n"] += 1
        return i

    for sq in range(nseq):
        for l in range(nl):
            tok0 = sq * S
            src_d = x_d if l == 0 else x1_d
            last = (l == nl - 1)
            cv = Carver()
            xin = [cv.take(D, F32) for _ in range(2)]
            xnb = [cv.take(D, BF16) for _ in range(2)]
            junk = cv.take(D, BF16)
            bA = [Buf("xin0"), Buf("xin1"), Buf("xn0"), Buf("xn1"), Buf("junkA"), Buf("smA")]
            inherit(bA, arena_bufs_prev)
            arena_bufs_prev = bA
            dma("sp", nwb[:, :], AP(nw_d, l * D, [[0, 128], [1, D]]), [], [B_nwb])
            for i in range(NT):
                s_ = i % 2
                dma("sp", xin[s_].ap(), src_d[tok0 + i * 128: tok0 + (i + 1) * 128, :], [B_x1[i]], [bA[s_]])
                ssq = small[:, s_:s_ + 1]
                rs = small[:, 2 + s_:3 + s_]
                act(junk.ap(), xin[s_].ap(), AF.Square, [bA[s_]], [bA[4], bA[5]], accum_out=ssq)
                act(rs, ssq, AF.Sqrt, [bA[5], B_const], [bA[5]], scale=1.0 / D, bias=epsb[:, :], ss=True)
                recip(rs, rs, [bA[5]], [bA[5]], ss=True)
                stt(xnb[s_].ap(), xin[s_].ap(), rs, nwb[:, :], ALU.mult, ALU.mult, [bA[s_], bA[5], B_nwb], [bA[2 + s_]], ss=True)
                for kt in range(8):
                    tr(pT[:, kt * 128:(kt + 1) * 128], xnb[s_].ap(kt * 128, [(1, 128)]), [bA[2 + s_]], [B_pT])
                cp("act", V(hT, i * 128, [(S, 8), (1, 128)]), V(pT, 0, [(128, 8), (1, 128)]), [B_pT], [B_hT[i]])

            if DBG and (DBGM & 1) and sq == 0 and l == int(os.environ.get("MK_DBGL", "0")):
                dma("sp", d_hT[:, :], V(hT, 0, [(1, 8 * S)]), B_hT, [])
            cv = Carver()
            R_STG, R_SQ, R_XW, R_T1, R_T2, R_QR, R_SS = 3, 3, 2, 4, 3, 2, 4
            stg = [cv.take(384, F32) for _ in range(R_STG)]
            sqr = [cv.take(320, F32) for _ in range(R_SQ)]
            xw = [cv.take(320, F32) for _ in range(R_XW)]
            t1 = [cv.take(320, F32) for _ in range(R_T1)]
            t2 = [cv.take(320, F32) for _ in range(R_T2)]
            qr = [cv.take(512, BF16) for _ in range(R_QR)]
            QKT = cv.take(4 * S, BF16)
            sag = cv.take(2 * S, BF16)
            PT = [cv.take(512, BF16) for _ in range(4)]
            Tt = [cv.take(512, F32) for _ in range(2)]
            bStg = [Buf("stg%d" % i) for i in range(R_STG)]
            bSq = [Buf("sq%d" % i) for i in range(R_SQ)]
            bSs = [Buf("ss%d" % i) for i in range(R_SS)]
            bXw = [Buf("xw%d" % i) for i in range(R_XW)]
            bT1 = [Buf("t1_%d" % i) for i in range(R_T1)]
            bT2 = [Buf("t2_%d" % i) for i in range(R_T2)]
            bQr = [Buf("qr%d" % i) for i in range(R_QR)]
            SSC = (8, 16, 24, 40)
            bQKT = [Buf("QKT%d" % i) for i in range(NT)]
            bSag = [[Buf("sag") for tb in range(4)] for b in range(2)]
            bPT = [Buf("PT%d" % i) for i in range(4)]
            bT = [Buf("T0"), Buf("T1")]
            bB = bStg + bSq + bSs + bXw + bT1 + bT2 + bQr + bQKT + bSag[0] + bSag[1] + bPT + bT
            inherit(bB, arena_bufs_prev)
            arena_bufs_prev = bB
            for s_ in range(R_QR):
                sc.op("pool", (lambda r_: (lambda e: e.memset(r_.ap(320, [(1, 128)]), 0.0)))(qr[s_]), [], [bQr[s_]])
            for g in range(4):
                sl, bw = load_w(l, 5 * g, 5)
                for b in range(2):
                    for tb in range(4):
                        pi = next_pb()
                        for kt in range(8):
                            mm(pbank[pi][:, :], wblk(sl, 3 + b, kt), hTs(kt, tb * 512, 512), kt == 0, kt == 7,
                               [bw] + B_hT[tb * 4:tb * 4 + 4], [B_pb[pi]])
                        act(sag.ap(b * S + tb * 512, [(1, 512)]), pbank[pi][:, :], AF.Silu, [B_pb[pi]], [bSag[b][tb]])
                def st1(i):
                    pi = next_pb()
                    for kt in range(8):
                        mm(pbank[pi][:, 0:384], hTs(kt, i * 128, 128), wrow(sl, 0, 3, kt), kt == 0, kt == 7,
                           [bw, B_hT[i]], [B_pb[pi]])
                    cp("act", stg[i % R_STG].ap(), pbank[pi][:, 0:384], [B_pb[pi]], [bStg[i % R_STG]])
                    act(sqr[i % R_SQ].ap(), pbank[pi][:, 0:320], AF.Square, [B_pb[pi]], [bSq[i % R_SQ]])

                def st2(i):
                    c0 = SSC[i % R_SS]
                    ssq = small[:, c0:c0 + 5]
                    sc.op("dve", (lambda o_, i_: (lambda e: e.tensor_reduce(out=o_, in_=i_, axis=AX.X, op=ALU.add)))(
                        ssq, sqr[i % R_SQ].ap(0, [(64, 5), (1, 64)])), [bSq[i % R_SQ]], [bSs[i % R_SS]])
                    act(ssq, ssq, AF.Sqrt, [bSs[i % R_SS], B_const], [bSs[i % R_SS]], scale=1.0 / 64, bias=epsb[:, :], ss=True)
                    tt("dve", xw[i % R_XW].ap(), stg[i % R_STG].ap(0, [(1, 320)]), wqkb[:, l, :], ALU.mult,
                       [bStg[i % R_STG], B_const], [bXw[i % R_XW]])
                    tt("pool", t1[i % R_T1].ap(0, [(64, 5), (1, 64)]), xw[i % R_XW].ap(0, [(64, 5), (1, 64)]),
                       V(ropeC, i * 64, [(0, 5), (1, 64)]), ALU.mult, [bXw[i % R_XW], B_const], [bT1[i % R_T1]])
                    for a in range(2):
                        tt("dve", t2[i % R_T2].ap(a * 32, [(64, 5), (16, 2), (1, 16)]),
                           xw[i % R_XW].ap(a * 32 + 16, [(64, 5), (-16, 2), (1, 16)]),
                           V(ropeS, i * 64 + a * 32, [(0, 5), (16, 2), (1, 16)]), ALU.mult, [bXw[i % R_XW], B_const], [bT2[i % R_T2]])
                    cp("pool", V(Va, i * 256, [(1, 64)]), stg[i % R_STG].ap(320, [(1, 64)]), [bStg[i % R_STG]], [B_Va[i]])
                    cp("pool", V(Va, i * 256 + 128 + 64, [(1, 64)]), stg[i % R_STG].ap(320, [(1, 64)]), [bStg[i % R_STG]], [B_Va[i]])

                def st3(i):
                    c0 = SSC[i % R_SS]
                    ssq = small[:, c0:c0 + 5]
                    recip(ssq, ssq, [bSs[i % R_SS]], [bSs[i % R_SS]], ss=True)
                    tt("pool", t1[i % R_T1].ap(), t1[i % R_T1].ap(), t2[i % R_T2].ap(), ALU.add, [bT1[i % R_T1], bT2[i % R_T2]], [bT1[i % R_T1]])

                def st4(i):
                    c0 = SSC[i % R_SS]
                    q_ = i % R_QR
                    tt("dve", qr[q_].ap(0, [(64, 5), (1, 64)]), t1[i % R_T1].ap(0, [(64, 5), (1, 64)]),
                       V(small, c0, [(1, 5), (0, 64)]), ALU.mult, [bT1[i % R_T1], bSs[i % R_SS]], [bQr[q_]], ss=True)
                    cp("pool", qr[q_].ap(448, [(1, 64)]), qr[q_].ap(256, [(1, 64)]), [bQr[q_]], [bQr[q_]])
                    for h in range(4):
                        tr(pT[:, h * 128:(h + 1) * 128], qr[q_].ap(h * 128, [(1, 128)]), [bQr[q_]], [B_pT])
                    cp("act", QKT.ap(i * 128, [(S, 4), (1, 128)]), V(pT, 0, [(128, 4), (1, 128)]), [B_pT], [bQKT[i]])

                for k in range(-3, NT):
                    for off, fn in ((3, st1), (2, st2), (1, st3), (0, st4)):
                        t_ = k + off
                        if 0 <= t_ < NT:
                            fn(t_)
                steps = [(h, tb, st) for h in range(4) for tb in range(4) for st in range(NT)]
                LA = 3
                PSS = (2, 3, 0, 1)

                def emit_qk(n):
                    h, tb, st = steps[n]
                    pq = PSS[n % 4]
                    mm(pbank[pq][:, :], QKT.ap((2 + h % 2) * S + st * 128, [(1, 128)]),
                       QKT.ap((h // 2) * S + tb * 512, [(1, 512)]), True, True,
                       [bQKT[st]] + bQKT[tb * 4:tb * 4 + 4], [B_pb[pq]])

                def emit_epi_pe(h, tb, tsl):
                    par = h % 2
                    b = h // 2
                    ft = 2 * g + b
                    orow = par * 64
                    mm(pbank[6][:, :], selsw, Tt[tsl].ap(), True, True, [bT[tsl], B_const], [B_pb[6]])
                    tt("dve", V(yaT, ft * S + tb * 512, [(1, 512)], orow, 64), Tt[tsl].ap(0, None, orow, 64),
                       pbank[6][orow:orow + 64, :], ALU.mult, [bT[tsl], B_pb[6]], [B_ya[ft][tb]])

                pend = []
                for n in range(LA):
                    emit_qk(n)
                for n, (h, tb, st) in enumerate(steps):
                    par = h % 2
                    b = h // 2
                    orow = par * 64
                    srow = 64 - orow
                    blk = n // NT
                    po = 4 + (blk % 2)
                    pq = PSS[n % 4]
                    pt = n % 4
                    act(PT[pt].ap(), pbank[pq][:, :], AF.Exp, [B_pb[pq]], [bPT[pt]], scale=0.125)
                    mm(pbank[po][:, :], V(Va, st * 256 + par * 128, [(1, 128)]), PT[pt].ap(), st == 0, st == NT - 1,
                       [bPT[pt], B_Va[st]], [B_pb[po]])
                    if n + LA < len(steps):
                        emit_qk(n + LA)
                    if st == NT - 1:
                        tsl = blk % 2
                        tt("dve", Tt[tsl].ap(0, None, orow, 64), pbank[po][orow:orow + 64, :],
                           sag.ap(b * S + tb * 512, [(1, 512)], orow, 64), ALU.mult, [B_pb[po], bSag[b][tb]], [bT[tsl]])
                        recip(Tt[tsl].ap(0, None, srow, 64), pbank[po][srow:srow + 64, :], [B_pb[po]], [bT[tsl]])
                        pend.append((n + 10, h, tb, tsl))
                    while pend and pend[0][0] <= n:
                        _, h_, tb_i, tsl_ = pend.pop(0)
                        emit_epi_pe(h_, tb_i, tsl_)
                for _, h_, tb_i, tsl_ in pend:
                    emit_epi_pe(h_, tb_i, tsl_)

            if DBG and (DBGM & 2) and sq == 0 and l == int(os.environ.get("MK_DBGL", "0")):
                dma("sp", d_ya[:, :], V(yaT, 0, [(1, 8 * S)]), [b for r_ in B_ya for b in r_], [])
            for ft in range(8):
                dma("sp", ya_d[ft], V(yaT, ft * S, [(1, S)]), B_ya[ft], [B_yad[ft]])
            cv = Carver()
            cv2 = Carver(1)
            t0 = cv.take(S, F32)
            tk = cv.take(S, BF16)
            tb_ = cv.take(S, F32)
            oT = cv.take(S, F32)
            QS = [cv.take(S, BF16), cv2.take(S, BF16)]
            VH = [cv.take(S, BF16), cv2.take(S, BF16)]
            SETS = []
            for c_ in (cv, cv2):
                SETS.append({"Qt": c_.take(S, BF16), "Kt": c_.take(S, BF16), "KhT": c_.take(S, BF16), "dcol": cv.take(64, F32),
                             "bQt": Buf("Qt"), "bKt": Buf("Kt"), "bKh": Buf("KhT"), "bdc": Buf("dcol")})
            HGS = [cv2.take(S, BF16), cv2.take(S, BF16)]
            nsq = [cv2.take(512, F32)] * 2
            nrs = [cv2.take(512, F32)] * 2
            Khm = [cv.take(4 * 128, BF16) for _ in range(2)]
            Zk = [cv.take(4 * 128, BF16) for _ in range(2)]
            Sf = [cv.take(4 * 128, F32) for _ in range(2)]
            Sb = [cv.take(4 * 128, BF16) for _ in range(4)]
            At = [cv.take(128, BF16) for _ in range(3)]
            zero_b = cv.take(128, BF16)
            bt0 = [Buf("t0_%d" % i) for i in range(4)]
            btk = Buf("tk")
            btb = Buf("tb")
            bqs = [[Buf("qs%d" % i) for i in range(4)] for _ in range(2)]
            boT = [Buf("oT%d" % i) for i in range(NT)]
            bhg = [[Buf("hg%d" % i) for i in range(4)] for _ in range(2)]
            bVh = [[Buf("Vh%d" % i) for i in range(4)] for _ in range(2)]
            bKhm = [Buf("Khm%d" % i) for i in range(2)]
            bZk = [Buf("Zk0"), Buf("Zk1")]
            bSf = [Buf("Sf0"), Buf("Sf1")]
            bSb = [Buf("Sb%d" % i) for i in range(4)]
            bAt = [Buf("At0"), Buf("At1"), Buf("At2")]
            bz = Buf("zero")
            _b1, _b2 = Buf("nsq"), Buf("nrs")
            bns = [_b1, _b1, _b2, _b2]
            bC1 = (bt0 + [btk, btb] + bqs[0] + boT + bVh[0] + [SETS[0][k] for k in ("bQt", "bKt", "bKh", "bdc")] + [SETS[1]["bdc"]]
                   + bKhm + bZk + bSf + bSb + bAt + [bz])
            bC2 = bqs[1] + bVh[1] + [SETS[1][k] for k in ("bQt", "bKt", "bKh")] + bhg[0] + bhg[1] + bns
            inherit(bC1, arena_bufs_prev)
            inherit(bC2, B_yad)
            arena_bufs_prev = bC1
            sc.op("pool", lambda e: e.memset(zero_b.ap(), 0.0), [], [bz])
            for z_ in range(2):
                sc.op("pool", (lambda r_: (lambda e: e.memset(r_.ap(), 0.0)))(Zk[z_]), [], [bZk[z_]])
            headw = {}

            def proj_units(j, blk, tb, evac):
                stp = {}
                u = []
                for q_ in range(4):
                    def f(q_=q_):
                        sl, bw = headw[j]
                        if q_ == 0:
                            stp["pi"] = next_pb()
                        pi = stp["pi"]
                        for kt in (2 * q_, 2 * q_ + 1):
                            mm(pbank[pi][:, :], wblk(sl, blk, kt), hTs(kt, tb * 512, 512), kt == 0, kt == 7,
                               [bw] + B_hT[tb * 4:tb * 4 + 4], [B_pb[pi]])
                        if q_ == 3:
                            evac(pi)
                    u.append(f)
                return u

            def P_units(j):
                par = j % 2
                u = []

                def ld():
                    headw[j] = load_w(l, 20 + 5 * j, 5)
                u.append(ld)
                for tb in range(4):
                    u += proj_units(j, 0, tb, lambda pi, tb=tb: act(QS[par].ap(tb * 512, [(1, 512)]), pbank[pi][:, :], AF.Silu,
                                                                     [B_pb[pi]], [bqs[par][tb]]))
                    stv = {}
                    for ii in range(4):
                        for hq_ in range(2):
                            def f_v(tb=tb, ii=ii, hq_=hq_, stv=stv):
                                sl, bw = headw[j]
                                if ii == 0 and hq_ == 0:
                                    stv["pi"] = next_pb()
                                pi = stv["pi"]
                                i = tb * 4 + ii
                                for kt in range(4 * hq_, 4 * hq_ + 4):
                                    mm(pbank[pi][:, ii * 128:(ii + 1) * 128], hTs(kt, i * 128, 128), wblk(sl, 2, kt), kt == 0, kt == 7,
                                       [bw, B_hT[i]], [B_pb[pi]])
                                if ii == 3 and hq_ == 1:
                                    cp("act", VH[par].ap(tb * 512, [(1, 512)]), pbank[pi][:, :], [B_pb[pi]], [bVh[par][tb]])
                            u.append(f_v)
                    u += proj_units(j, 1, tb, lambda pi, tb=tb: act(HGS[par].ap(tb * 512, [(1, 512)]), pbank[pi][:, :], AF.Silu,
                                                                     [B_pb[pi]], [bhg[par][tb]]))
                return u

            def prep_units(j, dr):
                par = j % 2
                X = SETS[dr]
                lbi = l * 16 + dr * 8 + j
                u = []
                for tb in range(4):
                    u += proj_units(j, 3 + dr, tb, lambda pi, tb=tb: act(t0.ap(tb * 512, [(1, 512)]), pbank[pi][:, :], AF.Sigmoid,
                                                                          [B_pb[pi]], [bt0[tb]]))
                u.append(lambda: ts("dve", t0.ap(), t0.ap(), omlt[:, lbi:lbi + 1], lbt[:, lbi:lbi + 1], ALU.mult, ALU.add, bt0 + [B_const], bt0))
                u.append(lambda: ts("pool", tk.ap(), t0.ap(), -1.0, 1.0, ALU.mult, ALU.add, bt0, [btk]))
                u.append(lambda: act(t0.ap(), t0.ap(), AF.Ln, bt0, bt0))
                if dr == 0:
                    u.append(lambda: sc.op("dve", lambda e: e.tensor_tensor_scan(out=tb_.ap(), data0=smask[:, :], data1=t0.ap(), initial=0.0,
                                                                                 op0=ALU.mult, op1=ALU.add), bt0 + [B_const], [btb]))
                else:
                    u.append(lambda: sc.op("dve", lambda e: e.tensor_tensor_scan(out=tb_.ap(S - 1, [(-1, S)]), data0=smask[:, :],
                                                                                 data1=t0.ap(S - 1, [(-1, S)]), initial=0.0,
                                                                                 op0=ALU.mult, op1=ALU.add), bt0 + [B_const], [btb]))
                u.append(lambda: act(t0.ap(), tb_.ap(), AF.Exp, [btb, B_const], bt0, bias=lncb[:, :]))
                u.append(lambda: tt("pool", X["Qt"].ap(), QS[par].ap(), t0.ap(), ALU.mult, bqs[par] + bt0, [X["bQt"]]))
                dpos = 31 if dr == 0 else 0
                u.append(lambda: act(X["dcol"].ap(), tb_.ap(dpos, [(32, 64)]), AF.Exp, [btb], [X["bdc"]]))
                u.append(lambda: act(t0.ap(), tb_.ap(), AF.Exp, [btb], bt0, scale=-1.0))
                u.append(lambda: tt("dve", t0.ap(), tk.ap(), t0.ap(), ALU.mult, [btk] + bt0, bt0))
                u.append(lambda: cp("act", X["Kt"].ap(), t0.ap(), bt0, [X["bKt"]]))
                u.append(lambda: tt("pool", X["KhT"].ap(0, [(32, 64), (1, 32)]), t0.ap(0, [(32, 64), (1, 32)]),
                                    X["dcol"].ap(0, [(1, 64), (0, 32)]), ALU.mult, bt0 + [X["bdc"]], [X["bKh"]]))
                return u

            def chain_units(j, dr):
                par = j % 2
                X = SETS[dr]
                Qt, Kt, KhT, dcol = X["Qt"], X["Kt"], X["KhT"], X["dcol"]
                bQt, bKt, bKh, bdc = X["bQt"], X["bKt"], X["bKh"], X["bdc"]
                Vh = VH[par]
                bV = bVh[par]
                tiles = list(range(NT)) if dr == 0 else list(range(NT - 1, -1, -1))
                mk = maskF if dr == 0 else maskB
                order = list(range(4)) if dr == 0 else list(range(3, -1, -1))
                st_ = {"cur": None}
                enter = {}

                def front(idx, i):
                    ib = i // 4
                    zs = idx % 2
                    for jj in range(4):
                        cp("pool", Zk[zs].ap(jj * 128 + jj * 32, [(1, 32)]), KhT.ap(i * 128 + jj * 32, [(1, 32)]), [bKh], [bZk[zs]])
                    for jj in range(4):
                        tr(pT[:, jj * 128:(jj + 1) * 128], Zk[zs].ap(jj * 128, [(1, 128)]), [bZk[zs]], [B_pT])
                    cp("act", Khm[zs].ap(), pT[:, 0:512], [B_pT], [bKhm[zs]])
                    mm(pbank[2][:, 0:128], Kt.ap(i * 128, [(1, 128)]), Qt.ap(i * 128, [(1, 128)]), True, True, [bKt, bQt], [B_pb[2]])
                    a_ = idx % 3
                    tt("dve", At[a_].ap(), pbank[2][:, 0:128], mk, ALU.mult, [B_pb[2], B_const], [bAt[a_]])
                    kb = (3, 6)[idx % 2]
                    for jj in order:
                        mm(pbank[kb][:, jj * 128:(jj + 1) * 128], Khm[zs].ap(jj * 128, [(1, 128)]), Vh.ap(i * 128, [(1, 128)]), True, True,
                           [bKhm[zs], bV[ib]], [B_pb[kb]])
                    hs = idx % 2
                    sbs = idx % 4
                    for n_, jj in enumerate(order):
                        c = i * 4 + jj
                        enter[c] = st_["cur"]
                        dst = Sf[hs].ap(n_ * 128, [(1, 128)])
                        if st_["cur"] is None:
                            cp("dve", dst, pbank[kb][:, jj * 128:(jj + 1) * 128], [B_pb[kb]], [bSf[hs]])
                        else:
                            ph, pn = st_["cur"]
                            stt(dst, Sf[ph % 2].ap(pn * 128, [(1, 128)]), dcol.ap(c, [(1, 1)]), pbank[kb][:, jj * 128:(jj + 1) * 128],
                                ALU.mult, ALU.add, [bSf[ph % 2], bdc, B_pb[kb]], [bSf[hs]])
                        st_["cur"] = (idx, n_)
                    cp("act", Sb[sbs].ap(), Sf[hs].ap(), [bSf[hs]], [bSb[sbs]])

                def back(idx, i):
                    ib = i // 4
                    a_ = idx % 3
                    po = 4 + (idx % 2)
                    mm(pbank[po][:, 0:128], Vh.ap(i * 128, [(1, 128)]), At[a_].ap(), True, False, [bV[ib], bAt[a_]], [B_pb[po]])
                    for n_, jj in enumerate(order):
                        c = i * 4 + jj
                        es_ = enter[c]
                        stat = zero_b.ap() if es_ is None else Sb[es_[0] % 4].ap(es_[1] * 128, [(1, 128)])
                        rb = [bz] if es_ is None else [bSb[es_[0] % 4]]
                        mm(pbank[po][:, jj * 32:(jj + 1) * 32], stat, Qt.ap(c * 32, [(1, 32)]), False, n_ == 3,
                           rb + [bQt], [B_pb[po]])
                    if dr == 0:
                        cp("act", oT.ap(i * 128, [(1, 128)]), pbank[po][:, 0:128], [B_pb[po]], [boT[i]])
                    else:
                        tt("dve", oT.ap(i * 128, [(1, 128)]), pbank[po][:, 0:128], oT.ap(i * 128, [(1, 128)]), ALU.add,
                           [B_pb[po], boT[i]], [boT[i]])

                u = [lambda: front(0, tiles[0]), lambda: front(1, tiles[1])]
                for idx, i in enumerate(tiles):
                    if idx + 2 < NT:
                        u.append(lambda idx=idx: front(idx + 2, tiles[idx + 2]))
                    u.append(lambda idx=idx, i=i: back(idx, i))
                return u

            def norm_units(j):
                u = []
                for tb in range(4):
                    def f_n(tb=tb):
                        n_ = tb % 2
                        oblk = oT.ap(tb * 512, [(1, 512)])
                        tt("pool", nsq[n_].ap(), oblk, oblk, ALU.mult, boT[tb * 4:tb * 4 + 4], [bns[n_]])
                        mm(pbank[6][:, :], onesf, nsq[n_].ap(), True, True, [bns[n_], B_const], [B_pb[6]])
                        act(nrs[n_].ap(), pbank[6][:, :], AF.Ln, [B_pb[6], B_const], [bns[2 + n_]], scale=1.0 / 128, bias=epsb[:, :])
                        act(nrs[n_].ap(), nrs[n_].ap(), AF.Exp, [bns[2 + n_]], [bns[2 + n_]], scale=-0.5)
                        tt("dve", nrs[n_].ap(), nrs[n_].ap(), oblk, ALU.mult, [bns[2 + n_]] + boT[tb * 4:tb * 4 + 4], [bns[2 + n_]])
                        stt(V(yhT, j * S + tb * 512, [(1, 512)]), nrs[n_].ap(), gw[:, l:l + 1], HGS[j % 2].ap(tb * 512, [(1, 512)]),
                            ALU.mult, ALU.mult, [bns[2 + n_], bhg[j % 2][tb], B_const], [B_yh[j][tb]])
                    u.append(f_n)
                return u

            def merge(primary, secondary):
                np_, ns_ = len(primary), len(secondary)
                si = 0
                for pi_, pu in enumerate(primary):
                    pu()
                    target = (pi_ + 1) * ns_ // np_
                    while si < target:
                        secondary[si]()
                        si += 1
                while si < ns_:
                    secondary[si]()
                    si += 1

            for u_ in P_units(0) + prep_units(0, 0):
                u_()
            for j in range(8):
                merge(chain_units(j, 0), prep_units(j, 1))
                sec = (P_units(j + 1) + prep_units(j + 1, 0)) if j < 7 else []
                merge(chain_units(j, 1) + norm_units(j), sec)
            inherit([b for r_ in B_ya for b in r_], bC2)
            for ft in range(8):
                dma("sp", V(yaT, ft * S, [(1, S)]), ya_d[ft], [B_yad[ft]], B_ya[ft])

            if DBG and (DBGM & 4) and sq == 0 and l == int(os.environ.get("MK_DBGL", "0")):
                dma("sp", d_yh[:, :], V(yhT, 0, [(1, 8 * S)]), [b for r_ in B_yh for b in r_], [])
            cv = Carver()
            wo = cv.take(8 * 1024, BF16)
            mT = cv.take(8 * 512, BF16)
            sa = [cv.take(512, F32) for _ in range(2)]
            sh = [cv.take(512, F32) for _ in range(2)]
            xr = [cv.take(D, F32) for _ in range(2)]
            xo = [cv.take(D, F32) for _ in range(2)]
            junkD = cv.take(D, BF16)
            bwo = Buf("wo")
            bmT = [Buf("mT%d" % i) for i in range(8)]
            bsa = [Buf("sa0"), Buf("sa1")]
            bsh = [Buf("sh0"), Buf("sh1")]
            bxr = [Buf("xr0"), Buf("xr1")]
            bxo = [Buf("xo0"), Buf("xo1")]
            bjk = Buf("junkD")
            bsm = Buf("smD")
            bD = [bwo] + bmT + bsa + bsh + bxr + bxo + [bjk, bsm]
            inherit(bD, arena_bufs_prev)
            arena_bufs_prev = bD
            if last:
                dma("sp", nwb[:, :], AP(fnw_d, 0, [[0, 128], [1, D]]), [], [B_nwb])
            dma("sp", wo.ap(0, [(1024, 8), (1, 1024)]), wsc_d[l, 92:100].rearrange("e p c -> p e c"),
                [B_wsc[l][p] for p in range(46, 50)], [bwo])
            for tb in range(4):
                for db in range(8):
                    sl, bw = load_w(l, 60 + 4 * db, 4)
                    s_ = db % 2
                    for which, dst, bdst in ((0, sa, bsa), (1, sh, bsh)):
                        pi = next_pb()
                        for kt in range(8):
                            mm(pbank[pi][:, :], wblk(sl, which, kt), hTs(kt, tb * 512, 512), kt == 0, kt == 7,
                               [bw] + B_hT[tb * 4:tb * 4 + 4], [B_pb[pi]])
                        act(dst[s_].ap(), pbank[pi][:, :], AF.Sigmoid, [B_pb[pi]], [bdst[s_]])
                    pi = next_pb()
                    for kt in range(8):
                        mm(pbank[pi][:, :], wblk(sl, 2, kt), V(yaT, kt * S + tb * 512, [(1, 512)]), kt == 0, kt == 7,
                           [bw, B_ya[kt][tb]], [B_pb[pi]])
                    tt("dve", sa[s_].ap(), pbank[pi][:, :], sa[s_].ap(), ALU.mult, [B_pb[pi], bsa[s_]], [bsa[s_]])
                    pi = next_pb()
                    for kt in range(8):
                        mm(pbank[pi][:, :], wblk(sl, 3, kt), V(yhT, kt * S + tb * 512, [(1, 512)]), kt == 0, kt == 7,
                           [bw, B_yh[kt][tb]], [B_pb[pi]])
                    tt("dve", sh[s_].ap(), pbank[pi][:, :], sh[s_].ap(), ALU.mult, [B_pb[pi], bsh[s_]], [bsh[s_]])
                    tt("pool", mT.ap(db * 512, [(1, 512)]), sa[s_].ap(), sh[s_].ap(), ALU.add, [bsa[s_], bsh[s_]], [bmT[db]])
                for ii in range(4):
                    i = tb * 4 + ii
                    s_ = i % 2
                    r0 = tok0 + i * 128
                    dma("sp", xr[s_].ap(), src_d[r0:r0 + 128, :], [B_x1[i]], [bxr[s_]])
                    for hf in range(2):
                        pi = next_pb()
                        for kt in range(8):
                            mm(pbank[pi][:, :], mT.ap(kt * 512 + ii * 128, [(1, 128)]),
                               wo.ap(hf * 4 * 1024 + kt * 128, [(1024, 4), (1, 128)]), kt == 0, kt == 7,
                               [bmT[kt], bwo], [B_pb[pi]])
                        tt("dve", xo[s_].ap(hf * 512, [(1, 512)]), pbank[pi][:, :], xr[s_].ap(hf * 512, [(1, 512)]), ALU.add,
                           [B_pb[pi], bxr[s_]], [bxo[s_]])
                    if DBG and (DBGM & 8) and sq == 0 and l == int(os.environ.get("MK_DBGL", "0")):
                        dma("sp", d_x1[i * 128:(i + 1) * 128, :], xo[s_].ap(), [bxo[s_]], [])
                    if not last:
                        dma("sp", x1_d[r0:r0 + 128, :], xo[s_].ap(), [bxo[s_]], [B_x1[i]])
                    else:
                        ssq = small[:, 32 + s_:33 + s_]
                        act(junkD.ap(), xo[s_].ap(), AF.Square, [bxo[s_]], [bjk, bsm], accum_out=ssq)
                        act(ssq, ssq, AF.Sqrt, [bsm, B_const], [bsm], scale=1.0 / D, bias=epsb[:, :], ss=True)
                        recip(ssq, ssq, [bsm], [bsm], ss=True)
                        stt(xo[s_].ap(), xo[s_].ap(), ssq, nwb[:, :], ALU.mult, ALU.mult, [bxo[s_], bsm, B_nwb], [bxo[s_]], ss=True)
                        dma("sp", out_d[r0:r0 + 128, :], xo[s_].ap(), [bxo[s_]], [])

    sc.finalize()
    sems = {}
    for s_ in Sched.STREAMS:
        n = (sc.nsig.get(s_, 0) + CH - 1) // CH
        sems[s_] = [es.enter_context(nc.semaphore("c_%s_%d" % (s_, i))) for i in range(max(n, 1))]
    for dom, n in sc.ndma.items():
        sems[dom] = [es.enter_context(nc.semaphore("%s_%d" % (dom, i))) for i in range(min(KD, n))]
    with nc.Block() as block:
        @block.tensor
        def _(e):
            sc.emit("pe", e, sems)

        @block.scalar
        def _(e):
            sc.emit("act", e, sems)

        @block.vector
        def _(e):
            sc.emit("dve", e, sems)

        @block.gpsimd
        def _(e):
            sc.emit("pool", e, sems)

        @block.sync
        def _(e):
            sc.emit("sp", e, sems, final_wait=True)
    es.close()
    return nc


def _perm_cols():
    cols = []
    for g in range(4):
        cols += list(range(O_AQ + 256 * g, O_AQ + 256 * g + 256))
        cols += list(range(O_AK + 64 * g, O_AK + 64 * g + 64))
        cols += list(range(O_AV + 64 * g, O_AV + 64 * g + 64))
        cols += list(range(O_AG + 256 * g, O_AG + 256 * g + 256))
    for j in range(8):
        for base in (O_HQ, O_HG, O_HI, O_HFF, O_HFB):
            cols += list(range(base + 128 * j, base + 128 * j + 128))
    return np.array(cols, dtype=np.int64)


def _blocks(w):
    n = w.shape[1] // 128
    return np.ascontiguousarray(w.reshape(8, 128, n, 128).transpose(2, 1, 0, 3)).reshape(n, 128, 1024)


def _host_consts():
    cm = np.zeros((6, 128, 128), np.float32)
    cm[0] = np.eye(128, dtype=np.float32)
    s_ = np.arange(128)[:, None]
    t_ = np.arange(128)[None, :]
    same = (s_ // 32) == (t_ // 32)
    cm[1] = (same & (s_ <= t_)).astype(np.float32)
    cm[2] = (same & (s_ >= t_)).astype(np.float32)
    cm[3] = (t_ == (s_ + 64) % 128).astype(np.float32)
    cm[4] = 1.0
    for jj in range(4):
        cm[5][32 * jj:32 * jj + 32, jj] = 1.0
    smask = np.ones((128, S), np.float32)
    smask[:, ::32] = 0.0
    pos = np.arange(S)
    row = (pos // 64).astype(np.float32)
    col = (pos % 64).astype(np.float32)
    inv = (10000.0 ** (-np.arange(0, 32, 2, dtype=np.float32) / 32)).astype(np.float32)
    ar = row[:, None] * inv
    ac = col[:, None] * inv
    C = np.concatenate([np.cos(ar), np.cos(ar), np.cos(ac), np.cos(ac)], axis=1).astype(np.float32)
    Sn = np.concatenate([-np.sin(ar), np.sin(ar), -np.sin(ac), np.sin(ac)], axis=1).astype(np.float32)
    rope = np.stack([C, Sn]).astype(np.float32)
    return cm, smask, rope


_NC_CACHE = {}


def kernel(x, w_in, norm_w, q_norm_w, k_norm_w, hgrn_lower_bounds, hgrn_norm_w,
           w_branch_attn, w_branch_hgrn, w_out, final_norm_w):
    ncores = 8
    nseq = int(os.environ.get("MK_NSEQ", "4"))
    nl = int(os.environ.get("MK_NL", "2"))
    x = np.asarray(x, np.float32)
    w_in = np.asarray(w_in, np.float32)
    perm = _perm_cols()
    wall = np.zeros((2, NBLK, 128, 1024), np.float32)
    for l in range(2):
        wall[l, 0:60] = _blocks(w_in[l][:, perm])
        ma = _blocks(w_in[l][:, O_MA:O_MA + 1024])
        mh = _blocks(w_in[l][:, O_MH:O_MH + 1024])
        wa = _blocks(np.asarray(w_branch_attn[l], np.float32))
        wh = _blocks(np.asarray(w_branch_hgrn[l], np.float32))
        for db in range(8):
            wall[l, 60 + 4 * db + 0] = ma[db]
            wall[l, 60 + 4 * db + 1] = mh[db]
            wall[l, 60 + 4 * db + 2] = wa[db]
            wall[l, 60 + 4 * db + 3] = wh[db]
        wall[l, 92:100] = _blocks(np.asarray(w_out[l], np.float32))
    cm, smask, rope = _host_consts()
    qw = np.asarray(q_norm_w, np.float32)
    kw = np.asarray(k_norm_w, np.float32)
    qkw = np.concatenate([np.tile(qw, (1, 4)), kw], axis=1).astype(np.float32)
    lb = np.asarray(hgrn_lower_bounds, np.float32)
    lbraw = np.ascontiguousarray(lb.reshape(2, 2, 8, 128).transpose(3, 1, 0, 2)).reshape(128, 32)
    gwh = np.ascontiguousarray(np.asarray(hgrn_norm_w, np.float32).T)
    nw = np.asarray(norm_w, np.float32)
    fnw = np.asarray(final_norm_w, np.float32).reshape(1, D)

    key = (nseq, nl)
    if key not in _NC_CACHE:
        _NC_CACHE[key] = build(nseq, nl)
    nc = _NC_CACHE[key]
    B = x.shape[0]
    per = B // ncores
    in_maps = []
    for c in range(ncores):
        xs = x[c * per: c * per + nseq].reshape(nseq * S, D)
        in_maps.append({"x": np.ascontiguousarray(xs), "wall": wall, "cmat": cm, "smask": smask, "rope": rope,
                        "nw": nw, "fnw": fnw, "qkw": qkw, "lbraw": lbraw, "gw": gwh})
    res = run_bass_kernel_spmd(nc, in_maps, core_ids=list(range(ncores)))
    if os.environ.get("MK_DBG", "0") == "1":
        kernel.dbg = {k: np.asarray(res.results[0][k]) for k in ("d_hT", "d_ya", "d_yh", "d_x1")}
    out = np.zeros((B, S, D), np.float32)
    for c in range(ncores):
        out[c * per: c * per + nseq] = np.asarray(res.results[c]["out"], np.float32).reshape(nseq, S, D)
    return out
```

```python
import os
import numpy as np
from contextlib import ExitStack
import concourse.bass as bass
import concourse.mybir as mybir
from concourse.bass_utils import run_bass_kernel_spmd
from concourse.ap import AP

F32 = mybir.dt.float32
BF16 = mybir.dt.bfloat16
AF = mybir.ActivationFunctionType
ALU = mybir.AluOpType
AX = mybir.AxisListType

S = 2048
D = 1024
NT = 16
NBLK = 100
EPS = 1e-6
CH = 4096
KD = 16

O_AQ, O_AK, O_AV, O_AG, O_HQ, O_HFF, O_HFB, O_HI, O_HG, O_MA, O_MH = (
    0, 1024, 1280, 1536, 2560, 3584, 4608, 5632, 6656, 7680, 8704)


class Buf:
    __slots__ = ("w", "r", "name")

    def __init__(self, name=""):
        self.w = None
        self.r = []
        self.name = name


class Op:
    __slots__ = ("stream", "dom", "fn", "deps", "sig", "idx", "isdma")


class Sched:
    STREAMS = ("pe", "act", "dve", "pool", "sp")

    def __init__(self):
        self.ops = {s: [] for s in self.STREAMS}
        self.nsig = {}
        self.ndma = {}

    def op(self, stream, fn, reads=(), writes=(), dma=False, ss=False):
        o = Op()
        o.stream = stream
        o.isdma = dma
        o.dom = ("dma_" + stream) if dma else stream
        o.fn = fn
        o.sig = 0
        deps = set()
        for b in reads:
            if b.w is not None:
                deps.add(b.w)
        for b in writes:
            if b.w is not None:
                deps.add(b.w)
            deps.update(b.r)
        for b in writes:
            b.w = o
            b.r = []
        for b in reads:
            if b.w is not o:
                b.r.append(o)
        deps.discard(o)
        need = []
        for d in deps:
            if d.isdma or dma or d.stream != stream or ss:
                need.append(d)
        o.deps = need
        if dma:
            n = self.ndma.get(o.dom, 0) + 1
            self.ndma[o.dom] = n
            o.idx = n
        self.ops[stream].append(o)
        return o

    def finalize(self):
        needed = set()
        for s in self.STREAMS:
            for o in self.ops[s]:
                for d in o.deps:
                    if not d.isdma:
                        needed.add(d)
        for s in self.STREAMS:
            k = 0
            for o in self.ops[s]:
                if (not o.isdma) and o in needed:
                    k += 1
                    o.sig = k
            self.nsig[s] = k

    def emit(self, stream, eng, sems, final_wait=False):
        waited_c = {}
        waited_d = {}
        for o in self.ops[stream]:
            wc = {}
            wd = {}
            for d in o.deps:
                if d.isdma:
                    si = (d.idx - 1) % KD
                    val = 16 * ((d.idx - 1) // KD + 1)
                    key = (d.dom, si)
                    if wd.get(key, 0) < val:
                        wd[key] = val
                else:
                    if wc.get(d.dom, 0) < d.sig:
                        wc[d.dom] = d.sig
            if o.isdma and o.idx > KD:
                key = (o.dom, (o.idx - 1) % KD)
                val = 16 * ((o.idx - 1) // KD)
                if wd.get(key, 0) < val:
                    wd[key] = val
            for dom, k in wc.items():
                if waited_c.get(dom, 0) >= k:
                    continue
                waited_c[dom] = k
                eng.wait_ge(sems[dom][(k - 1) // CH], (k - 1) % CH + 1)
            for key, val in wd.items():
                if waited_d.get(key, 0) >= val:
                    continue
                waited_d[key] = val
                eng.wait_ge(sems[key[0]][key[1]], val)
            ins = o.fn(eng)
            if o.isdma:
                ins.then_inc(sems[o.dom][(o.idx - 1) % KD], 16)
            elif o.sig:
                ins.then_inc(sems[o.dom][(o.sig - 1) // CH], 1)
        if final_wait:
            for dom, n in self.ndma.items():
                for si in range(min(KD, n)):
                    cnt = (n - 1 - si) // KD + 1
                    eng.wait_ge(sems[dom][si], 16 * cnt)


def V(t, off, dims, p0=0, npart=128):
    row = 1
    for d in t.shape[1:]:
        row *= d
    return AP(t, p0 * row + off, [[row, npart]] + [[int(s), int(c)] for s, c in dims])


def build(nseq, nl):
    nc = bass.Bass("TRN2", target_bir_lowering=False)
    T = nseq * S
    sc = Sched()
    es = ExitStack()

    def dram(name, shape, dt, kind="Internal"):
        return nc.dram_tensor(name, list(shape), dt, kind=kind)

    x_d = dram("x", [T, D], F32, "ExternalInput")
    wall_d = dram("wall", [2, NBLK, 128, 1024], F32, "ExternalInput")
    cm_d = dram("cmat", [6, 128, 128], F32, "ExternalInput")
    smask_d = dram("smask", [128, S], F32, "ExternalInput")
    rope_d = dram("rope", [2, S, 64], F32, "ExternalInput")
    nw_d = dram("nw", [2, D], F32, "ExternalInput")
    fnw_d = dram("fnw", [1, D], F32, "ExternalInput")
    qkw_d = dram("qkw", [2, 320], F32, "ExternalInput")
    lb_d = dram("lbraw", [128, 32], F32, "ExternalInput")
    gw_d = dram("gw", [128, 2], F32, "ExternalInput")
    out_d = dram("out", [T, D], F32, "ExternalOutput")
    wsc_d = dram("wsc", [2, NBLK, 128, 1024], BF16)
    x1_d = dram("x1s", [T, D], F32)
    ya_d = dram("yas", [8, 128, S], BF16)
    DBG = os.environ.get("MK_DBG", "0") == "1"
    DBGM = int(os.environ.get("MK_DBGM", "15"))
    if DBG:
        d_hT = dram("d_hT", [128, 8 * S], BF16, "ExternalOutput")
        d_ya = dram("d_ya", [128, 8 * S], BF16, "ExternalOutput")
        d_yh = dram("d_yh", [128, 8 * S], BF16, "ExternalOutput")
        d_x1 = dram("d_x1", [S, D], F32, "ExternalOutput")

    def sb(name, shape, dt):
        return es.enter_context(nc.sbuf_tensor(name, list(shape), dt))

    def ps(name, shape, dt):
        return es.enter_context(nc.psum_tensor(name, list(shape), dt))

    ident = sb("ident", [128, 128], BF16)
    cmat = sb("cmatsb", [128, 5, 128], F32)
    smask = sb("smasksb", [128, S], BF16)
    ropeC = sb("ropeC", [128, NT, 64], F32)
    ropeS = sb("ropeS", [128, NT, 64], F32)
    nwb = sb("nwb", [128, D], F32)
    wqkb = sb("wqkb", [128, 2, 320], F32)
    lbraw = sb("lbrawsb", [128, 32], F32)
    lbt = sb("lbt", [128, 32], F32)
    omlt = sb("omlt", [128, 32], F32)
    gw = sb("gwsb", [128, 2], F32)
    epsb = sb("epsb", [128, 1], F32)
    small = sb("small", [128, 64], F32)
    hT = sb("hT", [128, 8, S], BF16)
    yaT = sb("yaT", [128, 8, S], BF16)
    yhT = sb("yhT", [128, 8, S], BF16)
    Va = sb("Va", [128, NT, 2, 128], BF16)
    wring = sb("wring", [128, 2, 5, 1024], BF16)
    ARENA_BYTES = 63232
    arena = sb("arena", [128, ARENA_BYTES // 4], F32)
    pbank = [ps("pb%d" % i, [128, 512], F32) for i in range(7)]
    pT = ps("pbT", [128, 1024], BF16)

    B_const = Buf("const")
    B_small = {}

    arena_bf = arena.bitcast(BF16)
    ya_f32 = yaT.bitcast(F32)
    ARENAS = [(arena, arena_bf, ARENA_BYTES), (ya_f32, yaT, 8 * S * 2)]

    class Region:
        def __init__(self, off_bytes, n, dt, which=0):
            self.dt = dt
            self.n = n
            if dt == F32:
                self.t = ARENAS[which][0]
                self.off = off_bytes // 4
            else:
                self.t = ARENAS[which][1]
                self.off = off_bytes // 2

        def ap(self, off=0, dims=None, p0=0, npart=128):
            if dims is None:
                dims = [(1, self.n - off)]
            return V(self.t, self.off + off, dims, p0, npart)

    class Carver:
        def __init__(self, which=0):
            self.pos = 0
            self.which = which

        def take(self, n, dt):
            sz = n * (4 if dt == F32 else 2)
            r = Region(self.pos, n, dt, self.which)
            self.pos += (sz + 63) // 64 * 64
            assert self.pos <= ARENAS[self.which][2], ("arena overflow", self.which, self.pos)
            return r

    arena_bufs_prev = []

    def inherit(new_bufs, old_bufs):
        olds = []
        for b in old_bufs:
            if b.w is not None:
                olds.append(b.w)
            olds.extend(b.r)
        for nb in new_bufs:
            nb.r = list(olds)

    def mm(out, lhsT, rhs, start, stop, reads, writes):
        return sc.op("pe", lambda e: e.matmul(out, lhsT=lhsT, rhs=rhs, start=start, stop=stop), reads, writes)

    def tr(out, in_, reads, writes):
        return sc.op("pe", lambda e: e.transpose(out, in_, ident[:, :]), list(reads) + [B_const], writes)

    def act(out, in_, func, reads, writes, scale=1.0, bias=None, accum_out=None, ss=False):
        def f(e):
            kw = {}
            if bias is not None:
                kw["bias"] = bias
            if accum_out is not None:
                kw["accum_out"] = accum_out
            return e.activation(out=out, in_=in_, func=func, scale=scale, **kw)
        return sc.op("act", f, reads, writes, ss=ss)

    def tt(eng, out, in0, in1, op, reads, writes, ss=False):
        return sc.op(eng, lambda e: e.tensor_tensor(out=out, in0=in0, in1=in1, op=op), reads, writes, ss=ss)

    def ts(eng, out, in0, s1, s2, op0, op1, reads, writes):
        if op1 is None:
            return sc.op(eng, lambda e: e.tensor_scalar(out=out, in0=in0, scalar1=s1, scalar2=None, op0=op0), reads, writes)
        return sc.op(eng, lambda e: e.tensor_scalar(out=out, in0=in0, scalar1=s1, scalar2=s2, op0=op0, op1=op1), reads, writes)

    def stt(out, in0, scalar, in1, op0, op1, reads, writes, ss=False):
        return sc.op("dve", lambda e: e.scalar_tensor_tensor(out=out, in0=in0, scalar=scalar, in1=in1, op0=op0, op1=op1), reads, writes, ss=ss)

    def cp(eng, out, in_, reads, writes):
        if eng == "act":
            return act(out, in_, AF.Copy, reads, writes)
        return sc.op(eng, lambda e: e.tensor_copy(out=out, in_=in_), reads, writes)

    def recip(out, in_, reads, writes, ss=False):
        return sc.op("dve", lambda e: e.reciprocal(out=out, in_=in_), reads, writes, ss=ss)

    def dma(q, out, in_, reads, writes):
        return sc.op(q, lambda e: e.dma_start(out=out, in_=in_), reads, writes, dma=True)

    NPAIR = NBLK // 2
    B_wsc = [[Buf("wsc%d_%d" % (l, p)) for p in range(NPAIR)] for l in range(2)]
    for l in range(nl):
        for p in range(NPAIR):
            dma("pool", wsc_d[l, 2 * p:2 * p + 2].rearrange("a p c -> (a p) c"),
                wall_d[l, 2 * p:2 * p + 2].rearrange("a p c -> (a p) c"), [], [B_wsc[l][p]])
    dma("pool", ident[:, :], cm_d[0], [], [B_const])
    dma("pool", smask[:, :], smask_d[:, :], [], [B_const])
    dma("sp", cmat[:, :, :], cm_d[1:6].rearrange("a p c -> p a c"), [], [B_const])
    dma("sp", ropeC[:, :, :], rope_d[0].rearrange("(i p) c -> p i c", p=128), [], [B_const])
    dma("sp", ropeS[:, :, :], rope_d[1].rearrange("(i p) c -> p i c", p=128), [], [B_const])
    dma("sp", V(wqkb, 0, [(1, 640)]), AP(qkw_d, 0, [[0, 128], [1, 640]]), [], [B_const])
    dma("sp", lbraw[:, :], lb_d[:, :], [], [B_const])
    dma("sp", gw[:, :], gw_d[:, :], [], [B_const])
    sc.op("pool", lambda e: e.memset(Va[:, :, :, :], 1.0), [], [B_const])
    sc.op("dve", lambda e: e.memset(epsb[:, :], EPS), [], [B_const])
    sc.op("dve", lambda e: e.memset(lbt[:, :], 0.0), [], [B_const])
    tt("dve", lbt[:, 16:32], lbraw[:, 16:32], lbraw[:, 0:16], ALU.subtract, [B_const], [B_const])
    act(lbt[:, 16:32], lbt[:, 16:32], AF.Sigmoid, [B_const], [B_const])
    ts("dve", omlt[:, :], lbt[:, :], -1.0, 1.0, ALU.mult, ALU.add, [B_const], [B_const])

    maskF = cmat[:, 0, :]
    maskB = cmat[:, 1, :]
    selsw = cmat[:, 2, :]
    onesf = cmat[:, 3, :]
    rowmask = cmat[:, 4, :]

    B_hT = [Buf("hT%d" % i) for i in range(NT)]
    B_ya = [[Buf("ya%d_%d" % (f, tb)) for tb in range(4)] for f in range(8)]
    B_yh = [[Buf("yh%d_%d" % (j, tb)) for tb in range(4)] for j in range(8)]
    B_pb = [Buf("pb%d" % i) for i in range(7)]
    B_pT = Buf("pT")
    B_wr = [Buf("wr0"), Buf("wr1")]
    B_Va = [Buf("Va%d" % i) for i in range(NT)]
    B_nwb = Buf("nwb")
    B_yad = [Buf("yad%d" % i) for i in range(8)]
    B_x1 = [Buf("x1_%d" % i) for i in range(NT)]
    wr_state = {"n": 0}

    def load_w(l, b0, nb):
        sl = wr_state["n"] % 2
        wr_state["n"] += 1
        reads = [B_wsc[l][p] for p in range(b0 // 2, (b0 + nb - 1) // 2 + 1)]
        dma("sp", V(wring, sl * 5 * 1024, [(1024, nb), (1, 1024)]),
            wsc_d[l, b0:b0 + nb].rearrange("e p c -> p e c"), reads, [B_wr[sl]])
        return sl, B_wr[sl]

    def wblk(sl, b, kt, c0=0, ncol=128):
        return V(wring, sl * 5 * 1024 + b * 1024 + kt * 128 + c0, [(1, ncol)])

    def wrow(sl, b0, nb, kt):
        return V(wring, sl * 5 * 1024 + b0 * 1024 + kt * 128, [(1024, nb), (1, 128)])

    def hTs(kt, t0, n):
        return V(hT, kt * S + t0, [(1, n)])

    pbrr = {"n": 0}

    def next_pb(lo=0, hi=2):
        i = lo + pbrr["n"] % (hi - lo)
        pbrr["n"] += 1
        return i

    for sq in range(nseq):
        for l in range(nl):
            tok0 = sq * S
            src_d = x_d if l == 0 else x1_d
            last = (l == nl - 1)
            cv = Carver()
            xin = [cv.take(D, F32) for _ in range(2)]
            xnb = [cv.take(D, BF16) for _ in range(2)]
            junk = cv.take(D, BF16)
            bA = [Buf("xin0"), Buf("xin1"), Buf("xn0"), Buf("xn1"), Buf("junkA"), Buf("smA")]
            inherit(bA, arena_bufs_prev)
            arena_bufs_prev = bA
            dma("sp", nwb[:, :], AP(nw_d, l * D, [[0, 128], [1, D]]), [], [B_nwb])
            for i in range(NT):
                s_ = i % 2
                dma("sp", xin[s_].ap(), src_d[tok0 + i * 128: tok0 + (i + 1) * 128, :], [B_x1[i]], [bA[s_]])
                ssq = small[:, s_:s_ + 1]
                rs = small[:, 2 + s_:3 + s_]
                act(junk.ap(), xin[s_].ap(), AF.Square, [bA[s_]], [bA[4], bA[5]], accum_out=ssq)
                act(rs, ssq, AF.Sqrt, [bA[5], B_const], [bA[5]], scale=1.0 / D, bias=epsb[:, :], ss=True)
                recip(rs, rs, [bA[5]], [bA[5]], ss=True)
                stt(xnb[s_].ap(), xin[s_].ap(), rs, nwb[:, :], ALU.mult, ALU.mult, [bA[s_], bA[5], B_nwb], [bA[2 + s_]], ss=True)
                for kt in range(8):
                    tr(pT[:, kt * 128:(kt + 1) * 128], xnb[s_].ap(kt * 128, [(1, 128)]), [bA[2 + s_]], [B_pT])
                cp("act", V(hT, i * 128, [(S, 8), (1, 128)]), V(pT, 0, [(128, 8), (1, 128)]), [B_pT], [B_hT[i]])

            if DBG and (DBGM & 1) and sq == 0 and l == int(os.environ.get("MK_DBGL", "0")):
                dma("sp", d_hT[:, :], V(hT, 0, [(1, 8 * S)]), B_hT, [])
            cv = Carver()
            R_STG, R_SQ, R_XW, R_T1, R_T2, R_QR, R_SS = 3, 3, 2, 4, 3, 2, 4
            stg = [cv.take(384, F32) for _ in range(R_STG)]
            sqr = [cv.take(320, F32) for _ in range(R_SQ)]
            xw = [cv.take(320, F32) for _ in range(R_XW)]
            t1 = [cv.take(320, F32) for _ in range(R_T1)]
            t2 = [cv.take(320, F32) for _ in range(R_T2)]
            qr = [cv.take(512, BF16) for _ in range(R_QR)]
            QKT = cv.take(4 * S, BF16)
            sag = cv.take(2 * S, BF16)
            PT = [cv.take(512, BF16) for _ in range(4)]
            Tt = [cv.take(512, F32) for _ in range(2)]
            bStg = [Buf("stg%d" % i) for i in range(R_STG)]
            bSq = [Buf("sq%d" % i) for i in range(R_SQ)]
            bSs = [Buf("ss%d" % i) for i in range(R_SS)]
            bXw = [Buf("xw%d" % i) for i in range(R_XW)]
            bT1 = [Buf("t1_%d" % i) for i in range(R_T1)]
            bT2 = [Buf("t2_%d" % i) for i in range(R_T2)]
            bQr = [Buf("qr%d" % i) for i in range(R_QR)]
            SSC = (8, 16, 24, 40)
            bQKT = [Buf("QKT%d" % i) for i in range(NT)]
            bSag = [[Buf("sag") for tb in range(4)] for b in range(2)]
            bPT = [Buf("PT%d" % i) for i in range(4)]
            bT = [Buf("T0"), Buf("T1")]
            bB = bStg + bSq + bSs + bXw + bT1 + bT2 + bQr + bQKT + bSag[0] + bSag[1] + bPT + bT
            inherit(bB, arena_bufs_prev)
            arena_bufs_prev = bB
            for s_ in range(R_QR):
                sc.op("pool", (lambda r_: (lambda e: e.memset(r_.ap(320, [(1, 128)]), 0.0)))(qr[s_]), [], [bQr[s_]])
            for g in range(4):
                sl, bw = load_w(l, 5 * g, 5)
                for b in range(2):
                    for tb in range(4):
                        pi = next_pb()
                        for kt in range(8):
                            mm(pbank[pi][:, :], wblk(sl, 3 + b, kt), hTs(kt, tb * 512, 512), kt == 0, kt == 7,
                               [bw] + B_hT[tb * 4:tb * 4 + 4], [B_pb[pi]])
                        act(sag.ap(b * S + tb * 512, [(1, 512)]), pbank[pi][:, :], AF.Silu, [B_pb[pi]], [bSag[b][tb]])
                def st1(i):
                    pi = next_pb()
                    for kt in range(8):
                        mm(pbank[pi][:, 0:384], hTs(kt, i * 128, 128), wrow(sl, 0, 3, kt), kt == 0, kt == 7,
                           [bw, B_hT[i]], [B_pb[pi]])
                    cp("act", stg[i % R_STG].ap(), pbank[pi][:, 0:384], [B_pb[pi]], [bStg[i % R_STG]])
                    act(sqr[i % R_SQ].ap(), pbank[pi][:, 0:320], AF.Square, [B_pb[pi]], [bSq[i % R_SQ]])

                def st2(i):
                    c0 = SSC[i % R_SS]
                    ssq = small[:, c0:c0 + 5]
                    sc.op("dve", (lambda o_, i_: (lambda e: e.tensor_reduce(out=o_, in_=i_, axis=AX.X, op=ALU.add)))(
                        ssq, sqr[i % R_SQ].ap(0, [(64, 5), (1, 64)])), [bSq[i % R_SQ]], [bSs[i % R_SS]])
                    act(ssq, ssq, AF.Sqrt, [bSs[i % R_SS], B_const], [bSs[i % R_SS]], scale=1.0 / 64, bias=epsb[:, :], ss=True)
                    tt("dve", xw[i % R_XW].ap(), stg[i % R_STG].ap(0, [(1, 320)]), wqkb[:, l, :], ALU.mult,
                       [bStg[i % R_STG], B_const], [bXw[i % R_XW]])
                    tt("pool", t1[i % R_T1].ap(0, [(64, 5), (1, 64)]), xw[i % R_XW].ap(0, [(64, 5), (1, 64)]),
                       V(ropeC, i * 64, [(0, 5), (1, 64)]), ALU.mult, [bXw[i % R_XW], B_const], [bT1[i % R_T1]])
                    for a in range(2):
                        tt("dve", t2[i % R_T2].ap(a * 32, [(64, 5), (16, 2), (1, 16)]),
                           xw[i % R_XW].ap(a * 32 + 16, [(64, 5), (-16, 2), (1, 16)]),
                           V(ropeS, i * 64 + a * 32, [(0, 5), (16, 2), (1, 16)]), ALU.mult, [bXw[i % R_XW], B_const], [bT2[i % R_T2]])
                    cp("pool", V(Va, i * 256, [(1, 64)]), stg[i % R_STG].ap(320, [(1, 64)]), [bStg[i % R_STG]], [B_Va[i]])
                    cp("pool", V(Va, i * 256 + 128 + 64, [(1, 64)]), stg[i % R_STG].ap(320, [(1, 64)]), [bStg[i % R_STG]], [B_Va[i]])

                def st3(i):
                    c0 = SSC[i % R_SS]
                    ssq = small[:, c0:c0 + 5]
                    recip(ssq, ssq, [bSs[i % R_SS]], [bSs[i % R_SS]], ss=True)
                    tt("pool", t1[i % R_T1].ap(), t1[i % R_T1].ap(), t2[i % R_T2].ap(), ALU.add, [bT1[i % R_T1], bT2[i % R_T2]], [bT1[i % R_T1]])

                def st4(i):
                    c0 = SSC[i % R_SS]
                    q_ = i % R_QR
                    tt("dve", qr[q_].ap(0, [(64, 5), (1, 64)]), t1[i % R_T1].ap(0, [(64, 5), (1, 64)]),
                       V(small, c0, [(1, 5), (0, 64)]), ALU.mult, [bT1[i % R_T1], bSs[i % R_SS]], [bQr[q_]], ss=True)
                    cp("pool", qr[q_].ap(448, [(1, 64)]), qr[q_].ap(256, [(1, 64)]), [bQr[q_]], [bQr[q_]])
                    for h in range(4):
                        tr(pT[:, h * 128:(h + 1) * 128], qr[q_].ap(h * 128, [(1, 128)]), [bQr[q_]], [B_pT])
                    cp("act", QKT.ap(i * 128, [(S, 4), (1, 128)]), V(pT, 0, [(128, 4), (1, 128)]), [B_pT], [bQKT[i]])

                for k in range(-3, NT):
                    for off, fn in ((3, st1), (2, st2), (1, st3), (0, st4)):
                        t_ = k + off
                        if 0 <= t_ < NT:
                            fn(t_)
                steps = [(h, tb, st) for h in range(4) for tb in range(4) for st in range(NT)]
                LA = 3
                PSS = (2, 3, 0, 1)

                def emit_qk(n):
                    h, tb, st = steps[n]
                    pq = PSS[n % 4]
                    mm(pbank[pq][:, :], QKT.ap((2 + h % 2) * S + st * 128, [(1, 128)]),
                       QKT.ap((h // 2) * S + tb * 512, [(1, 512)]), True, True,
                       [bQKT[st]] + bQKT[tb * 4:tb * 4 + 4], [B_pb[pq]])

                def emit_epi_pe(h, tb, tsl):
                    par = h % 2
                    b = h // 2
                    ft = 2 * g + b
                    orow = par * 64
                    mm(pbank[6][:, :], selsw, Tt[tsl].ap(), True, True, [bT[tsl], B_const], [B_pb[6]])
                    tt("dve", V(yaT, ft * S + tb * 512, [(1, 512)], orow, 64), Tt[tsl].ap(0, None, orow, 64),
                       pbank[6][orow:orow + 64, :], ALU.mult, [bT[tsl], B_pb[6]], [B_ya[ft][tb]])

                pend = []
                for n in range(LA):
                    emit_qk(n)
                for n, (h, tb, st) in enumerate(steps):
                    par = h % 2
                    b = h // 2
                    orow = par * 64
                    srow = 64 - orow
                    blk = n // NT
                    po = 4 + (blk % 2)
                    pq = PSS[n % 4]
                    pt = n % 4
                    act(PT[pt].ap(), pbank[pq][:, :], AF.Exp, [B_pb[pq]], [bPT[pt]], scale=0.125)
                    mm(pbank[po][:, :], V(Va, st * 256 + par * 128, [(1, 128)]), PT[pt].ap(), st == 0, st == NT - 1,
                       [bPT[pt], B_Va[st]], [B_pb[po]])
                    if n + LA < len(steps):
                        emit_qk(n + LA)
                    if st == NT - 1:
                        tsl = blk % 2
                        tt("dve", Tt[tsl].ap(0, None, orow, 64), pbank[po][orow:orow + 64, :],
                           sag.ap(b * S + tb * 512, [(1, 512)], orow, 64), ALU.mult, [B_pb[po], bSag[b][tb]], [bT[tsl]])
                        recip(Tt[tsl].ap(0, None, srow, 64), pbank[po][srow:srow + 64, :], [B_pb[po]], [bT[tsl]])
                        pend.append((n + 10, h, tb, tsl))
                    while pend and pend[0][0] <= n:
                        _, h_, tb_i, tsl_ = pend.pop(0)
                        emit_epi_pe(h_, tb_i, tsl_)
                for _, h_, tb_i, tsl_ in pend:
                    emit_epi_pe(h_, tb_i, tsl_)

            if DBG and (DBGM & 2) and sq == 0 and l == int(os.environ.get("MK_DBGL", "0")):
                dma("sp", d_ya[:, :], V(yaT, 0, [(1, 8 * S)]), [b for r_ in B_ya for b in r_], [])
            for ft in range(8):
                dma("sp", ya_d[ft], V(yaT, ft * S, [(1, S)]), B_ya[ft], [B_yad[ft]])
            cv = Carver()
            cv2 = Carver(1)
            t0 = cv.take(S, F32)
            tk = cv.take(S, BF16)
            tb_ = cv.take(S, F32)
            oT = cv.take(S, F32)
            QS = [cv.take(S, BF16), cv2.take(S, BF16)]
            VH = [cv.take(S, BF16), cv2.take(S, BF16)]
            SETS = []
            for c_ in (cv, cv2):
                SETS.append({"Qt": c_.take(S, BF16), "Kt": c_.take(S, BF16), "KhT": c_.take(S, BF16), "dcol": cv.take(64, F32),
                             "bQt": Buf("Qt"), "bKt": Buf("Kt"), "bKh": Buf("KhT"), "bdc": Buf("dcol")})
            HGS = [cv2.take(S, BF16), cv2.take(S, BF16)]
            nsq = [cv2.take(512, F32)] * 2
            nrs = [cv2.take(512, F32)] * 2
            Khm = [cv.take(4 * 128, BF16) for _ in range(2)]
            Zk = [cv.take(4 * 128, BF16) for _ in range(2)]
            Sf = [cv.take(4 * 128, F32) for _ in range(2)]
            Sb = [cv.take(4 * 128, BF16) for _ in range(4)]
            At = [cv.take(128, BF16) for _ in range(3)]
            zero_b = cv.take(128, BF16)
            bt0 = [Buf("t0_%d" % i) for i in range(4)]
            btk = Buf("tk")
            btb = Buf("tb")
            bqs = [[Buf("qs%d" % i) for i in range(4)] for _ in range(2)]
            boT = [Buf("oT%d" % i) for i in range(NT)]
            bhg = [[Buf("hg%d" % i) for i in range(4)] for _ in range(2)]
            bVh = [[Buf("Vh%d" % i) for i in range(4)] for _ in range(2)]
            bKhm = [Buf("Khm%d" % i) for i in range(2)]
            bZk = [Buf("Zk0"), Buf("Zk1")]
            bSf = [Buf("Sf0"), Buf("Sf1")]
            bSb = [Buf("Sb%d" % i) for i in range(4)]
            bAt = [Buf("At0"), Buf("At1"), Buf("At2")]
            bz = Buf("zero")
            _b1, _b2 = Buf("nsq"), Buf("nrs")
            bns = [_b1, _b1, _b2, _b2]
            bC1 = (bt0 + [btk, btb] + bqs[0] + boT + bVh[0] + [SETS[0][k] for k in ("bQt", "bKt", "bKh", "bdc")] + [SETS[1]["bdc"]]
                   + bKhm + bZk + bSf + bSb + bAt + [bz])
            bC2 = bqs[1] + bVh[1] + [SETS[1][k] for k in ("bQt", "bKt", "bKh")] + bhg[0] + bhg[1] + bns
            inherit(bC1, arena_bufs_prev)
            inherit(bC2, B_yad)
            arena_bufs_prev = bC1
            sc.op("pool", lambda e: e.memset(zero_b.ap(), 0.0), [], [bz])
            for z_ in range(2):
                sc.op("pool", (lambda r_: (lambda e: e.memset(r_.ap(), 0.0)))(Zk[z_]), [], [bZk[z_]])
            headw = {}

            def proj_units(j, blk, tb, evac):
                stp = {}
                u = []
                for q_ in range(4):
                    def f(q_=q_):
                        sl, bw = headw[j]
                        if q_ == 0:
                            stp["pi"] = next_pb()
                        pi = stp["pi"]
                        for kt in (2 * q_, 2 * q_ + 1):
                            mm(pbank[pi][:, :], wblk(sl, blk, kt), hTs(kt, tb * 512, 512), kt == 0, kt == 7,
                               [bw] + B_hT[tb * 4:tb * 4 + 4], [B_pb[pi]])
                        if q_ == 3:
                            evac(pi)
                    u.append(f)
                return u

            def P_units(j):
                par = j % 2
                u = []

                def ld():
                    headw[j] = load_w(l, 20 + 5 * j, 5)
                u.append(ld)
                for tb in range(4):
                    u += proj_units(j, 0, tb, lambda pi, tb=tb: act(QS[par].ap(tb * 512, [(1, 512)]), pbank[pi][:, :], AF.Silu,
                                                                     [B_pb[pi]], [bqs[par][tb]]))
                    stv = {}
                    for ii in range(4):
                        for hq_ in range(2):
                            def f_v(tb=tb, ii=ii, hq_=hq_, stv=stv):
                                sl, bw = headw[j]
                                if ii == 0 and hq_ == 0:
                                    stv["pi"] = next_pb()
                                pi = stv["pi"]
                                i = tb * 4 + ii
                                for kt in range(4 * hq_, 4 * hq_ + 4):
                                    mm(pbank[pi][:, ii * 128:(ii + 1) * 128], hTs(kt, i * 128, 128), wblk(sl, 2, kt), kt == 0, kt == 7,
                                       [bw, B_hT[i]], [B_pb[pi]])
                                if ii == 3 and hq_ == 1:
                                    cp("act", VH[par].ap(tb * 512, [(1, 512)]), pbank[pi][:, :], [B_pb[pi]], [bVh[par][tb]])
                            u.append(f_v)
                    u += proj_units(j, 1, tb, lambda pi, tb=tb: act(HGS[par].ap(tb * 512, [(1, 512)]), pbank[pi][:, :], AF.Silu,
                                                                     [B_pb[pi]], [bhg[par][tb]]))
                return u

            def prep_units(j, dr):
                par = j % 2
                X = SETS[dr]
                lbi = l * 16 + dr * 8 + j
                u = []
                for tb in range(4):
                    u += proj_units(j, 3 + dr, tb, lambda pi, tb=tb: act(t0.ap(tb * 512, [(1, 512)]), pbank[pi][:, :], AF.Sigmoid,
                                                                          [B_pb[pi]], [bt0[tb]]))
                u.append(lambda: ts("dve", t0.ap(), t0.ap(), omlt[:, lbi:lbi + 1], lbt[:, lbi:lbi + 1], ALU.mult, ALU.add, bt0 + [B_const], bt0))
                u.append(lambda: ts("pool", tk.ap(), t0.ap(), -1.0, 1.0, ALU.mult, ALU.add, bt0, [btk]))
                u.append(lambda: act(t0.ap(), t0.ap(), AF.Ln, bt0, bt0))
                if dr == 0:
                    u.append(lambda: sc.op("dve", lambda e: e.tensor_tensor_scan(out=tb_.ap(), data0=smask[:, :], data1=t0.ap(), initial=0.0,
                                                                                 op0=ALU.mult, op1=ALU.add), bt0 + [B_const], [btb]))
                else:
                    u.append(lambda: sc.op("dve", lambda e: e.tensor_tensor_scan(out=tb_.ap(S - 1, [(-1, S)]), data0=smask[:, :],
                                                                                 data1=t0.ap(S - 1, [(-1, S)]), initial=0.0,
                                                                                 op0=ALU.mult, op1=ALU.add), bt0 + [B_const], [btb]))
                u.append(lambda: act(t0.ap(), tb_.ap(), AF.Exp, [btb], bt0))
                u.append(lambda: stt(X["Qt"].ap(), QS[par].ap(), 128.0 ** -0.5, t0.ap(), ALU.mult, ALU.mult, bqs[par] + bt0, [X["bQt"]]))
                dpos = 31 if dr == 0 else 0
                u.append(lambda: act(X["dcol"].ap(), tb_.ap(dpos, [(32, 64)]), AF.Exp, [btb], [X["bdc"]]))
                u.append(lambda: act(t0.ap(), tb_.ap(), AF.Exp, [btb], bt0, scale=-1.0))
                u.append(lambda: tt("dve", t0.ap(), tk.ap(), t0.ap(), ALU.mult, [btk] + bt0, bt0))
                u.append(lambda: cp("act", X["Kt"].ap(), t0.ap(), bt0, [X["bKt"]]))
                u.append(lambda: tt("dve", X["KhT"].ap(0, [(32, 64), (1, 32)]), t0.ap(0, [(32, 64), (1, 32)]),
                                    X["dcol"].ap(0, [(1, 64), (0, 32)]), ALU.mult, bt0 + [X["bdc"]], [X["bKh"]]))
                return u

            def chain_units(j, dr):
                par = j % 2
                X = SETS[dr]
                Qt, Kt, KhT, dcol = X["Qt"], X["Kt"], X["KhT"], X["dcol"]
                bQt, bKt, bKh, bdc = X["bQt"], X["bKt"], X["bKh"], X["bdc"]
                Vh = VH[par]
                bV = bVh[par]
                tiles = list(range(NT)) if dr == 0 else list(range(NT - 1, -1, -1))
                mk = maskF if dr == 0 else maskB
                order = list(range(4)) if dr == 0 else list(range(3, -1, -1))
                st_ = {"cur": None}
                enter = {}

                def front(idx, i):
                    ib = i // 4
                    zs = idx % 2
                    for jj in range(4):
                        cp("pool", Zk[zs].ap(jj * 128 + jj * 32, [(1, 32)]), KhT.ap(i * 128 + jj * 32, [(1, 32)]), [bKh], [bZk[zs]])
                    for jj in range(4):
                        tr(pT[:, jj * 128:(jj + 1) * 128], Zk[zs].ap(jj * 128, [(1, 128)]), [bZk[zs]], [B_pT])
                    cp("act", Khm[zs].ap(), pT[:, 0:512], [B_pT], [bKhm[zs]])
                    mm(pbank[2][:, 0:128], Kt.ap(i * 128, [(1, 128)]), Qt.ap(i * 128, [(1, 128)]), True, True, [bKt, bQt], [B_pb[2]])
                    a_ = idx % 3
                    tt("dve", At[a_].ap(), pbank[2][:, 0:128], mk, ALU.mult, [B_pb[2], B_const], [bAt[a_]])
                    kb = (3, 6)[idx % 2]
                    for jj in order:
                        mm(pbank[kb][:, jj * 128:(jj + 1) * 128], Khm[zs].ap(jj * 128, [(1, 128)]), Vh.ap(i * 128, [(1, 128)]), True, True,
                           [bKhm[zs], bV[ib]], [B_pb[kb]])
                    hs = idx % 2
                    sbs = idx % 4
                    for n_, jj in enumerate(order):
                        c = i * 4 + jj
                        enter[c] = st_["cur"]
                        dst = Sf[hs].ap(n_ * 128, [(1, 128)])
                        if st_["cur"] is None:
                            cp("dve", dst, pbank[kb][:, jj * 128:(jj + 1) * 128], [B_pb[kb]], [bSf[hs]])
                        else:
                            ph, pn = st_["cur"]
                            stt(dst, Sf[ph % 2].ap(pn * 128, [(1, 128)]), dcol.ap(c, [(1, 1)]), pbank[kb][:, jj * 128:(jj + 1) * 128],
                                ALU.mult, ALU.add, [bSf[ph % 2], bdc, B_pb[kb]], [bSf[hs]])
                        st_["cur"] = (idx, n_)
                    cp("act", Sb[sbs].ap(), Sf[hs].ap(), [bSf[hs]], [bSb[sbs]])

                def back(idx, i):
                    ib = i // 4
                    a_ = idx % 3
                    po = 4 + (idx % 2)
                    mm(pbank[po][:, 0:128], Vh.ap(i * 128, [(1, 128)]), At[a_].ap(), True, False, [bV[ib], bAt[a_]], [B_pb[po]])
                    for n_, jj in enumerate(order):
                        c = i * 4 + jj
                        es_ = enter[c]
                        stat = zero_b.ap() if es_ is None else Sb[es_[0] % 4].ap(es_[1] * 128, [(1, 128)])
                        rb = [bz] if es_ is None else [bSb[es_[0] % 4]]
                        mm(pbank[po][:, jj * 32:(jj + 1) * 32], stat, Qt.ap(c * 32, [(1, 32)]), False, n_ == 3,
                           rb + [bQt], [B_pb[po]])
                    if dr == 0:
                        cp("act", oT.ap(i * 128, [(1, 128)]), pbank[po][:, 0:128], [B_pb[po]], [boT[i]])
                    else:
                        tt("dve", oT.ap(i * 128, [(1, 128)]), pbank[po][:, 0:128], oT.ap(i * 128, [(1, 128)]), ALU.add,
                           [B_pb[po], boT[i]], [boT[i]])

                u = [lambda: front(0, tiles[0]), lambda: front(1, tiles[1])]
                for idx, i in enumerate(tiles):
                    if idx + 2 < NT:
                        u.append(lambda idx=idx: front(idx + 2, tiles[idx + 2]))
                    u.append(lambda idx=idx, i=i: back(idx, i))
                return u

            def norm_units(j):
                u = []
                for tb in range(4):
                    def f_n(tb=tb):
                        n_ = tb % 2
                        oblk = oT.ap(tb * 512, [(1, 512)])
                        tt("pool", nsq[n_].ap(), oblk, oblk, ALU.mult, boT[tb * 4:tb * 4 + 4], [bns[n_]])
                        mm(pbank[6][:, :], onesf, nsq[n_].ap(), True, True, [bns[n_], B_const], [B_pb[6]])
                        act(nrs[n_].ap(), pbank[6][:, :], AF.Ln, [B_pb[6], B_const], [bns[2 + n_]], scale=1.0 / 128, bias=epsb[:, :])
                        act(nrs[n_].ap(), nrs[n_].ap(), AF.Exp, [bns[2 + n_]], [bns[2 + n_]], scale=-0.5)
                        tt("dve", nrs[n_].ap(), nrs[n_].ap(), oblk, ALU.mult, [bns[2 + n_]] + boT[tb * 4:tb * 4 + 4], [bns[2 + n_]])
                        stt(V(yhT, j * S + tb * 512, [(1, 512)]), nrs[n_].ap(), gw[:, l:l + 1], HGS[j % 2].ap(tb * 512, [(1, 512)]),
                            ALU.mult, ALU.mult, [bns[2 + n_], bhg[j % 2][tb], B_const], [B_yh[j][tb]])
                    u.append(f_n)
                return u

            def merge(primary, secondary):
                np_, ns_ = len(primary), len(secondary)
                si = 0
                for pi_, pu in enumerate(primary):
                    pu()
                    target = (pi_ + 1) * ns_ // np_
                    while si < target:
                        secondary[si]()
                        si += 1
                while si < ns_:
                    secondary[si]()
                    si += 1

            for u_ in P_units(0) + prep_units(0, 0):
                u_()
            for j in range(8):
                merge(chain_units(j, 0), prep_units(j, 1))
                sec = (P_units(j + 1) + prep_units(j + 1, 0)) if j < 7 else []
                merge(chain_units(j, 1) + norm_units(j), sec)
            inherit([b for r_ in B_ya for b in r_], bC2)
            for ft in range(8):
                dma("sp", V(yaT, ft * S, [(1, S)]), ya_d[ft], [B_yad[ft]], B_ya[ft])

            if DBG and (DBGM & 4) and sq == 0 and l == int(os.environ.get("MK_DBGL", "0")):
                dma("sp", d_yh[:, :], V(yhT, 0, [(1, 8 * S)]), [b for r_ in B_yh for b in r_], [])
            cv = Carver()
            wo = cv.take(8 * 1024, BF16)
            mT = cv.take(8 * 512, BF16)
            sa = [cv.take(512, F32) for _ in range(2)]
            sh = [cv.take(512, F32) for _ in range(2)]
            xr = [cv.take(D, F32) for _ in range(2)]
            xo = [cv.take(D, F32) for _ in range(2)]
            junkD = cv.take(D, BF16)
            bwo = Buf("wo")
            bmT = [Buf("mT%d" % i) for i in range(8)]
            bsa = [Buf("sa0"), Buf("sa1")]
            bsh = [Buf("sh0"), Buf("sh1")]
            bxr = [Buf("xr0"), Buf("xr1")]
            bxo = [Buf("xo0"), Buf("xo1")]
            bjk = Buf("junkD")
            bsm = Buf("smD")
            bD = [bwo] + bmT + bsa + bsh + bxr + bxo + [bjk, bsm]
            inherit(bD, arena_bufs_prev)
            arena_bufs_prev = bD
            if last:
                dma("sp", nwb[:, :], AP(fnw_d, 0, [[0, 128], [1, D]]), [], [B_nwb])
            dma("sp", wo.ap(0, [(1024, 8), (1, 1024)]), wsc_d[l, 92:100].rearrange("e p c -> p e c"),
                [B_wsc[l][p] for p in range(46, 50)], [bwo])
            for tb in range(4):
                for db in range(8):
                    sl, bw = load_w(l, 60 + 4 * db, 4)
                    s_ = db % 2
                    for which, dst, bdst in ((0, sa, bsa), (1, sh, bsh)):
                        pi = next_pb()
                        for kt in range(8):
                            mm(pbank[pi][:, :], wblk(sl, which, kt), hTs(kt, tb * 512, 512), kt == 0, kt == 7,
                               [bw] + B_hT[tb * 4:tb * 4 + 4], [B_pb[pi]])
                        act(dst[s_].ap(), pbank[pi][:, :], AF.Sigmoid, [B_pb[pi]], [bdst[s_]])
                    pi = next_pb()
                    for kt in range(8):
                        mm(pbank[pi][:, :], wblk(sl, 2, kt), V(yaT, kt * S + tb * 512, [(1, 512)]), kt == 0, kt == 7,
                           [bw, B_ya[kt][tb]], [B_pb[pi]])
                    tt("dve", sa[s_].ap(), pbank[pi][:, :], sa[s_].ap(), ALU.mult, [B_pb[pi], bsa[s_]], [bsa[s_]])
                    pi = next_pb()
                    for kt in range(8):
                        mm(pbank[pi][:, :], wblk(sl, 3, kt), V(yhT, kt * S + tb * 512, [(1, 512)]), kt == 0, kt == 7,
                           [bw, B_yh[kt][tb]], [B_pb[pi]])
                    tt("dve", sh[s_].ap(), pbank[pi][:, :], sh[s_].ap(), ALU.mult, [B_pb[pi], bsh[s_]], [bsh[s_]])
                    tt("pool", mT.ap(db * 512, [(1, 512)]), sa[s_].ap(), sh[s_].ap(), ALU.add, [bsa[s_], bsh[s_]], [bmT[db]])
                for ii in range(4):
                    i = tb * 4 + ii
                    s_ = i % 2
                    r0 = tok0 + i * 128
                    dma("sp", xr[s_].ap(), src_d[r0:r0 + 128, :], [B_x1[i]], [bxr[s_]])
                    for hf in range(2):
                        pi = next_pb()
                        for kt in range(8):
                            mm(pbank[pi][:, :], mT.ap(kt * 512 + ii * 128, [(1, 128)]),
                               wo.ap(hf * 4 * 1024 + kt * 128, [(1024, 4), (1, 128)]), kt == 0, kt == 7,
                               [bmT[kt], bwo], [B_pb[pi]])
                        tt("dve", xo[s_].ap(hf * 512, [(1, 512)]), pbank[pi][:, :], xr[s_].ap(hf * 512, [(1, 512)]), ALU.add,
                           [B_pb[pi], bxr[s_]], [bxo[s_]])
                    if DBG and (DBGM & 8) and sq == 0 and l == int(os.environ.get("MK_DBGL", "0")):
                        dma("sp", d_x1[i * 128:(i + 1) * 128, :], xo[s_].ap(), [bxo[s_]], [])
                    if not last:
                        dma("sp", x1_d[r0:r0 + 128, :], xo[s_].ap(), [bxo[s_]], [B_x1[i]])
                    else:
                        ssq = small[:, 32 + s_:33 + s_]
                        act(junkD.ap(), xo[s_].ap(), AF.Square, [bxo[s_]], [bjk, bsm], accum_out=ssq)
                        act(ssq, ssq, AF.Sqrt, [bsm, B_const], [bsm], scale=1.0 / D, bias=epsb[:, :], ss=True)
                        recip(ssq, ssq, [bsm], [bsm], ss=True)
                        stt(xo[s_].ap(), xo[s_].ap(), ssq, nwb[:, :], ALU.mult, ALU.mult, [bxo[s_], bsm, B_nwb], [bxo[s_]], ss=True)
                        dma("sp", out_d[r0:r0 + 128, :], xo[s_].ap(), [bxo[s_]], [])

    sc.finalize()
    sems = {}
    for s_ in Sched.STREAMS:
        n = (sc.nsig.get(s_, 0) + CH - 1) // CH
        sems[s_] = [es.enter_context(nc.semaphore("c_%s_%d" % (s_, i))) for i in range(max(n, 1))]
    for dom, n in sc.ndma.items():
        sems[dom] = [es.enter_context(nc.semaphore("%s_%d" % (dom, i))) for i in range(min(KD, n))]
    with nc.Block() as block:
        @block.tensor
        def _(e):
            sc.emit("pe", e, sems)

        @block.scalar
        def _(e):
            sc.emit("act", e, sems)

        @block.vector
        def _(e):
            sc.emit("dve", e, sems)

        @block.gpsimd
        def _(e):
            sc.emit("pool", e, sems)

        @block.sync
        def _(e):
            sc.emit("sp", e, sems, final_wait=True)
    es.close()
    return nc


def _perm_cols():
    cols = []
    for g in range(4):
        cols += list(range(O_AQ + 256 * g, O_AQ + 256 * g + 256))
        cols += list(range(O_AK + 64 * g, O_AK + 64 * g + 64))
        cols += list(range(O_AV + 64 * g, O_AV + 64 * g + 64))
        cols += list(range(O_AG + 256 * g, O_AG + 256 * g + 256))
    for j in range(8):
        for base in (O_HQ, O_HG, O_HI, O_HFF, O_HFB):
            cols += list(range(base + 128 * j, base + 128 * j + 128))
    return np.array(cols, dtype=np.int64)


def _blocks(w):
    n = w.shape[1] // 128
    return np.ascontiguousarray(w.reshape(8, 128, n, 128).transpose(2, 1, 0, 3)).reshape(n, 128, 1024)


def _host_consts():
    cm = np.zeros((6, 128, 128), np.float32)
    cm[0] = np.eye(128, dtype=np.float32)
    s_ = np.arange(128)[:, None]
    t_ = np.arange(128)[None, :]
    same = (s_ // 32) == (t_ // 32)
    cm[1] = (same & (s_ <= t_)).astype(np.float32)
    cm[2] = (same & (s_ >= t_)).astype(np.float32)
    cm[3] = (t_ == (s_ + 64) % 128).astype(np.float32)
    cm[4] = 1.0
    for jj in range(4):
        cm[5][32 * jj:32 * jj + 32, jj] = 1.0
    smask = np.ones((128, S), np.float32)
    smask[:, ::32] = 0.0
    pos = np.arange(S)
    row = (pos // 64).astype(np.float32)
    col = (pos % 64).astype(np.float32)
    inv = (10000.0 ** (-np.arange(0, 32, 2, dtype=np.float32) / 32)).astype(np.float32)
    ar = row[:, None] * inv
    ac = col[:, None] * inv
    C = np.concatenate([np.cos(ar), np.cos(ar), np.cos(ac), np.cos(ac)], axis=1).astype(np.float32)
    Sn = np.concatenate([-np.sin(ar), np.sin(ar), -np.sin(ac), np.sin(ac)], axis=1).astype(np.float32)
    rope = np.stack([C, Sn]).astype(np.float32)
    return cm, smask, rope


_NC_CACHE = {}


def kernel(x, w_in, norm_w, q_norm_w, k_norm_w, hgrn_lower_bounds, hgrn_norm_w,
           w_branch_attn, w_branch_hgrn, w_out, final_norm_w):
    ncores = 8
    nseq = int(os.environ.get("MK_NSEQ", "4"))
    nl = int(os.environ.get("MK_NL", "2"))
    x = np.asarray(x, np.float32)
    w_in = np.asarray(w_in, np.float32)
    perm = _perm_cols()
    wall = np.zeros((2, NBLK, 128, 1024), np.float32)
    for l in range(2):
        wall[l, 0:60] = _blocks(w_in[l][:, perm])
        ma = _blocks(w_in[l][:, O_MA:O_MA + 1024])
        mh = _blocks(w_in[l][:, O_MH:O_MH + 1024])
        wa = _blocks(np.asarray(w_branch_attn[l], np.float32))
        wh = _blocks(np.asarray(w_branch_hgrn[l], np.float32))
        for db in range(8):
            wall[l, 60 + 4 * db + 0] = ma[db]
            wall[l, 60 + 4 * db + 1] = mh[db]
            wall[l, 60 + 4 * db + 2] = wa[db]
            wall[l, 60 + 4 * db + 3] = wh[db]
        wall[l, 92:100] = _blocks(np.asarray(w_out[l], np.float32))
    cm, smask, rope = _host_consts()
    qw = np.asarray(q_norm_w, np.float32)
    kw = np.asarray(k_norm_w, np.float32)
    qkw = np.concatenate([np.tile(qw, (1, 4)), kw], axis=1).astype(np.float32)
    lb = np.asarray(hgrn_lower_bounds, np.float32)
    lbraw = np.ascontiguousarray(lb.reshape(2, 2, 8, 128).transpose(3, 1, 0, 2)).reshape(128, 32)
    gwh = np.ascontiguousarray(np.asarray(hgrn_norm_w, np.float32).T)
    nw = np.asarray(norm_w, np.float32)
    fnw = np.asarray(final_norm_w, np.float32).reshape(1, D)

    key = (nseq, nl)
    if key not in _NC_CACHE:
        _NC_CACHE[key] = build(nseq, nl)
    nc = _NC_CACHE[key]
    B = x.shape[0]
    per = B // ncores
    in_maps = []
    for c in range(ncores):
        xs = x[c * per: c * per + nseq].reshape(nseq * S, D)
        in_maps.append({"x": np.ascontiguousarray(xs), "wall": wall, "cmat": cm, "smask": smask, "rope": rope,
                        "nw": nw, "fnw": fnw, "qkw": qkw, "lbraw": lbraw, "gw": gwh})
    res = run_bass_kernel_spmd(nc, in_maps, core_ids=list(range(ncores)))
    if os.environ.get("MK_DBG", "0") == "1":
        kernel.dbg = {k: np.asarray(res.results[0][k]) for k in ("d_hT", "d_ya", "d_yh", "d_x1")}
    out = np.zeros((B, S, D), np.float32)
    for c in range(ncores):
        out[c * per: c * per + nseq] = np.asarray(res.results[c]["out"], np.float32).reshape(nseq, S, D)
    return out
```

```python
import os
import numpy as np
from contextlib import ExitStack
import concourse.bass as bass
import concourse.mybir as mybir
from concourse.bass_utils import run_bass_kernel_spmd
from concourse.ap import AP

F32 = mybir.dt.float32
BF16 = mybir.dt.bfloat16
AF = mybir.ActivationFunctionType
ALU = mybir.AluOpType
AX = mybir.AxisListType

S = 2048
D = 1024
NT = 16
NBLK = 100
EPS = 1e-6
CH = 4096
KD = 16

O_AQ, O_AK, O_AV, O_AG, O_HQ, O_HFF, O_HFB, O_HI, O_HG, O_MA, O_MH = (
    0, 1024, 1280, 1536, 2560, 3584, 4608, 5632, 6656, 7680, 8704)


class Buf:
    __slots__ = ("w", "r", "name")

    def __init__(self, name=""):
        self.w = None
        self.r = []
        self.name = name


class Op:
    __slots__ = ("stream", "dom", "fn", "deps", "sig", "idx", "isdma")


class Sched:
    STREAMS = ("pe", "act", "dve", "pool", "sp")

    def __init__(self):
        self.ops = {s: [] for s in self.STREAMS}
        self.nsig = {}
        self.ndma = {}

    def op(self, stream, fn, reads=(), writes=(), dma=False, ss=False):
        o = Op()
        o.stream = stream
        o.isdma = dma
        o.dom = ("dma_" + stream) if dma else stream
        o.fn = fn
        o.sig = 0
        deps = set()
        for b in reads:
            if b.w is not None:
                deps.add(b.w)
        for b in writes:
            if b.w is not None:
                deps.add(b.w)
            deps.update(b.r)
        for b in writes:
            b.w = o
            b.r = []
        for b in reads:
            if b.w is not o:
                b.r.append(o)
        deps.discard(o)
        need = []
        for d in deps:
            if d.isdma or dma or d.stream != stream or ss:
                need.append(d)
        o.deps = need
        if dma:
            n = self.ndma.get(o.dom, 0) + 1
            self.ndma[o.dom] = n
            o.idx = n
        self.ops[stream].append(o)
        return o

    def finalize(self):
        needed = set()
        for s in self.STREAMS:
            for o in self.ops[s]:
                for d in o.deps:
                    if not d.isdma:
                        needed.add(d)
        for s in self.STREAMS:
            k = 0
            for o in self.ops[s]:
                if (not o.isdma) and o in needed:
                    k += 1
                    o.sig = k
            self.nsig[s] = k

    def emit(self, stream, eng, sems, final_wait=False):
        waited_c = {}
        waited_d = {}
        for o in self.ops[stream]:
            wc = {}
            wd = {}
            for d in o.deps:
                if d.isdma:
                    si = (d.idx - 1) % KD
                    val = 16 * ((d.idx - 1) // KD + 1)
                    key = (d.dom, si)
                    if wd.get(key, 0) < val:
                        wd[key] = val
                else:
                    if wc.get(d.dom, 0) < d.sig:
                        wc[d.dom] = d.sig
            if o.isdma and o.idx > KD:
                key = (o.dom, (o.idx - 1) % KD)
                val = 16 * ((o.idx - 1) // KD)
                if wd.get(key, 0) < val:
                    wd[key] = val
            for dom, k in wc.items():
                if waited_c.get(dom, 0) >= k:
                    continue
                waited_c[dom] = k
                eng.wait_ge(sems[dom][(k - 1) // CH], (k - 1) % CH + 1)
            for key, val in wd.items():
                if waited_d.get(key, 0) >= val:
                    continue
                waited_d[key] = val
                eng.wait_ge(sems[key[0]][key[1]], val)
            ins = o.fn(eng)
            if o.isdma:
                ins.then_inc(sems[o.dom][(o.idx - 1) % KD], 16)
            elif o.sig:
                ins.then_inc(sems[o.dom][(o.sig - 1) // CH], 1)
        if final_wait:
            for dom, n in self.ndma.items():
                for si in range(min(KD, n)):
                    cnt = (n - 1 - si) // KD + 1
                    eng.wait_ge(sems[dom][si], 16 * cnt)


def V(t, off, dims, p0=0, npart=128):
    row = 1
    for d in t.shape[1:]:
        row *= d
    return AP(t, p0 * row + off, [[row, npart]] + [[int(s), int(c)] for s, c in dims])


def build(nseq, nl):
    nc = bass.Bass("TRN2", target_bir_lowering=False)
    T = nseq * S
    sc = Sched()
    es = ExitStack()

    def dram(name, shape, dt, kind="Internal"):
        return nc.dram_tensor(name, list(shape), dt, kind=kind)

    x_d = dram("x", [T, D], F32, "ExternalInput")
    wall_d = dram("wall", [2, NBLK, 128, 1024], F32, "ExternalInput")
    cm_d = dram("cmat", [6, 128, 128], F32, "ExternalInput")
    smask_d = dram("smask", [128, S], F32, "ExternalInput")
    rope_d = dram("rope", [2, S, 64], F32, "ExternalInput")
    nw_d = dram("nw", [2, D], F32, "ExternalInput")
    fnw_d = dram("fnw", [1, D], F32, "ExternalInput")
    qkw_d = dram("qkw", [2, 320], F32, "ExternalInput")
    lb_d = dram("lbraw", [128, 32], F32, "ExternalInput")
    gw_d = dram("gw", [128, 2], F32, "ExternalInput")
    out_d = dram("out", [T, D], F32, "ExternalOutput")
    wsc_d = dram("wsc", [2, NBLK, 128, 1024], BF16)
    x1_d = dram("x1s", [T, D], F32)
    ya_d = dram("yas", [8, 128, S], BF16)
    DBG = os.environ.get("MK_DBG", "0") == "1"
    DBGM = int(os.environ.get("MK_DBGM", "15"))
    if DBG:
        d_hT = dram("d_hT", [128, 8 * S], BF16, "ExternalOutput")
        d_ya = dram("d_ya", [128, 8 * S], BF16, "ExternalOutput")
        d_yh = dram("d_yh", [128, 8 * S], BF16, "ExternalOutput")
        d_x1 = dram("d_x1", [S, D], F32, "ExternalOutput")

    def sb(name, shape, dt):
        return es.enter_context(nc.sbuf_tensor(name, list(shape), dt))

    def ps(name, shape, dt):
        return es.enter_context(nc.psum_tensor(name, list(shape), dt))

    ident = sb("ident", [128, 128], BF16)
    cmat = sb("cmatsb", [128, 5, 128], F32)
    smask = sb("smasksb", [128, S], BF16)
    ropeC = sb("ropeC", [128, NT, 64], F32)
    ropeS = sb("ropeS", [128, NT, 64], F32)
    nwb = sb("nwb", [128, D], F32)
    wqkb = sb("wqkb", [128, 2, 320], F32)
    lbraw = sb("lbrawsb", [128, 32], F32)
    lbt = sb("lbt", [128, 32], F32)
    omlt = sb("omlt", [128, 32], F32)
    gw = sb("gwsb", [128, 2], F32)
    epsb = sb("epsb", [128, 1], F32)
    small = sb("small", [128, 64], F32)
    hT = sb("hT", [128, 8, S], BF16)
    yaT = sb("yaT", [128, 8, S], BF16)
    yhT = sb("yhT", [128, 8, S], BF16)
    Va = sb("Va", [128, NT, 2, 128], BF16)
    wring = sb("wring", [128, 2, 5, 1024], BF16)
    ARENA_BYTES = 63232
    arena = sb("arena", [128, ARENA_BYTES // 4], F32)
    pbank = [ps("pb%d" % i, [128, 512], F32) for i in range(7)]
    pT = ps("pbT", [128, 1024], BF16)

    B_const = Buf("const")
    B_small = {}

    arena_bf = arena.bitcast(BF16)
    ya_f32 = yaT.bitcast(F32)
    ARENAS = [(arena, arena_bf, ARENA_BYTES), (ya_f32, yaT, 8 * S * 2)]

    class Region:
        def __init__(self, off_bytes, n, dt, which=0):
            self.dt = dt
            self.n = n
            if dt == F32:
                self.t = ARENAS[which][0]
                self.off = off_bytes // 4
            else:
                self.t = ARENAS[which][1]
                self.off = off_bytes // 2

        def ap(self, off=0, dims=None, p0=0, npart=128):
            if dims is None:
                dims = [(1, self.n - off)]
            return V(self.t, self.off + off, dims, p0, npart)

    class Carver:
        def __init__(self, which=0):
            self.pos = 0
            self.which = which

        def take(self, n, dt):
            sz = n * (4 if dt == F32 else 2)
            r = Region(self.pos, n, dt, self.which)
            self.pos += (sz + 63) // 64 * 64
            assert self.pos <= ARENAS[self.which][2], ("arena overflow", self.which, self.pos)
            return r

    arena_bufs_prev = []

    def inherit(new_bufs, old_bufs):
        olds = []
        for b in old_bufs:
            if b.w is not None:
                olds.append(b.w)
            olds.extend(b.r)
        for nb in new_bufs:
            nb.r = list(olds)

    def mm(out, lhsT, rhs, start, stop, reads, writes):
        return sc.op("pe", lambda e: e.matmul(out, lhsT=lhsT, rhs=rhs, start=start, stop=stop), reads, writes)

    def tr(out, in_, reads, writes):
        return sc.op("pe", lambda e: e.transpose(out, in_, ident[:, :]), list(reads) + [B_const], writes)

    def act(out, in_, func, reads, writes, scale=1.0, bias=None, accum_out=None, ss=False):
        def f(e):
            kw = {}
            if bias is not None:
                kw["bias"] = bias
            if accum_out is not None:
                kw["accum_out"] = accum_out
            return e.activation(out=out, in_=in_, func=func, scale=scale, **kw)
        return sc.op("act", f, reads, writes, ss=ss)

    def tt(eng, out, in0, in1, op, reads, writes, ss=False):
        return sc.op(eng, lambda e: e.tensor_tensor(out=out, in0=in0, in1=in1, op=op), reads, writes, ss=ss)

    def ts(eng, out, in0, s1, s2, op0, op1, reads, writes):
        if op1 is None:
            return sc.op(eng, lambda e: e.tensor_scalar(out=out, in0=in0, scalar1=s1, scalar2=None, op0=op0), reads, writes)
        return sc.op(eng, lambda e: e.tensor_scalar(out=out, in0=in0, scalar1=s1, scalar2=s2, op0=op0, op1=op1), reads, writes)

    def stt(out, in0, scalar, in1, op0, op1, reads, writes, ss=False):
        return sc.op("dve", lambda e: e.scalar_tensor_tensor(out=out, in0=in0, scalar=scalar, in1=in1, op0=op0, op1=op1), reads, writes, ss=ss)

    def cp(eng, out, in_, reads, writes):
        if eng == "act":
            return act(out, in_, AF.Copy, reads, writes)
        return sc.op(eng, lambda e: e.tensor_copy(out=out, in_=in_), reads, writes)

    def recip(out, in_, reads, writes, ss=False):
        return sc.op("dve", lambda e: e.reciprocal(out=out, in_=in_), reads, writes, ss=ss)

    def dma(q, out, in_, reads, writes):
        return sc.op(q, lambda e: e.dma_start(out=out, in_=in_), reads, writes, dma=True)

    NPAIR = NBLK // 2
    B_wsc = [[Buf("wsc%d_%d" % (l, p)) for p in range(NPAIR)] for l in range(2)]
    dma("pool", ident[:, :], cm_d[0], [], [B_const])
    dma("pool", smask[:, :], smask_d[:, :], [], [B_const])
    dma("sp", cmat[:, :, :], cm_d[1:6].rearrange("a p c -> p a c"), [], [B_const])
    dma("sp", ropeC[:, :, :], rope_d[0].rearrange("(i p) c -> p i c", p=128), [], [B_const])
    dma("sp", ropeS[:, :, :], rope_d[1].rearrange("(i p) c -> p i c", p=128), [], [B_const])
    dma("sp", V(wqkb, 0, [(1, 640)]), AP(qkw_d, 0, [[0, 128], [1, 640]]), [], [B_const])
    dma("sp", lbraw[:, :], lb_d[:, :], [], [B_const])
    dma("sp", gw[:, :], gw_d[:, :], [], [B_const])
    sc.op("pool", lambda e: e.memset(Va[:, :, :, :], 1.0), [], [B_const])
    sc.op("dve", lambda e: e.memset(epsb[:, :], EPS), [], [B_const])
    sc.op("dve", lambda e: e.memset(lbt[:, :], 0.0), [], [B_const])
    tt("dve", lbt[:, 16:32], lbraw[:, 16:32], lbraw[:, 0:16], ALU.subtract, [B_const], [B_const])
    act(lbt[:, 16:32], lbt[:, 16:32], AF.Sigmoid, [B_const], [B_const])
    ts("dve", omlt[:, :], lbt[:, :], -1.0, 1.0, ALU.mult, ALU.add, [B_const], [B_const])
    for l in range(nl):
        for p in range(NPAIR):
            dma("pool", wsc_d[l, 2 * p:2 * p + 2].rearrange("a p c -> (a p) c"),
                wall_d[l, 2 * p:2 * p + 2].rearrange("a p c -> (a p) c"), [], [B_wsc[l][p]])

    maskF = cmat[:, 0, :]
    maskB = cmat[:, 1, :]
    selsw = cmat[:, 2, :]
    onesf = cmat[:, 3, :]
    rowmask = cmat[:, 4, :]

    B_hT = [Buf("hT%d" % i) for i in range(NT)]
    B_ya = [[Buf("ya%d_%d" % (f, tb)) for tb in range(4)] for f in range(8)]
    B_yh = [[Buf("yh%d_%d" % (j, tb)) for tb in range(4)] for j in range(8)]
    B_pb = [Buf("pb%d" % i) for i in range(7)]
    B_pT = Buf("pT")
    B_wr = [Buf("wr0"), Buf("wr1")]
    B_Va = [Buf("Va%d" % i) for i in range(NT)]
    B_nwb = Buf("nwb")
    B_yad = [Buf("yad%d" % i) for i in range(8)]
    B_x1 = [Buf("x1_%d" % i) for i in range(NT)]
    wr_state = {"n": 0}

    def load_w(l, b0, nb):
        sl = wr_state["n"] % 2
        wr_state["n"] += 1
        reads = [B_wsc[l][p] for p in range(b0 // 2, (b0 + nb - 1) // 2 + 1)]
        dma("sp", V(wring, sl * 5 * 1024, [(1024, nb), (1, 1024)]),
            wsc_d[l, b0:b0 + nb].rearrange("e p c -> p e c"), reads, [B_wr[sl]])
        return sl, B_wr[sl]

    def wblk(sl, b, kt, c0=0, ncol=128):
        return V(wring, sl * 5 * 1024 + b * 1024 + kt * 128 + c0, [(1, ncol)])

    def wrow(sl, b0, nb, kt):
        return V(wring, sl * 5 * 1024 + b0 * 1024 + kt * 128, [(1024, nb), (1, 128)])

    def hTs(kt, t0, n):
        return V(hT, kt * S + t0, [(1, n)])

    pbrr = {"n": 0}

    def next_pb(lo=0, hi=2):
        i = lo + pbrr["n"] % (hi - lo)
        pbrr["n"] += 1
        return i

    for sq in range(nseq):
        for l in range(nl):
            tok0 = sq * S
            src_d = x_d if l == 0 else x1_d
            last = (l == nl - 1)
            cv = Carver()
            xin = [cv.take(D, F32) for _ in range(2)]
            xnb = [cv.take(D, BF16) for _ in range(2)]
            junk = cv.take(D, BF16)
            bA = [Buf("xin0"), Buf("xin1"), Buf("xn0"), Buf("xn1"), Buf("junkA"), Buf("smA")]
            inherit(bA, arena_bufs_prev)
            arena_bufs_prev = bA
            dma("sp", nwb[:, :], AP(nw_d, l * D, [[0, 128], [1, D]]), [], [B_nwb])
            for i in range(NT):
                s_ = i % 2
                dma("sp", xin[s_].ap(), src_d[tok0 + i * 128: tok0 + (i + 1) * 128, :], [B_x1[i]], [bA[s_]])
                ssq = small[:, s_:s_ + 1]
                rs = small[:, 2 + s_:3 + s_]
                act(junk.ap(), xin[s_].ap(), AF.Square, [bA[s_]], [bA[4], bA[5]], accum_out=ssq)
                act(rs, ssq, AF.Sqrt, [bA[5], B_const], [bA[5]], scale=1.0 / D, bias=epsb[:, :], ss=True)
                recip(rs, rs, [bA[5]], [bA[5]], ss=True)
                stt(xnb[s_].ap(), xin[s_].ap(), rs, nwb[:, :], ALU.mult, ALU.mult, [bA[s_], bA[5], B_nwb], [bA[2 + s_]], ss=True)
                for kt in range(8):
                    tr(pT[:, kt * 128:(kt + 1) * 128], xnb[s_].ap(kt * 128, [(1, 128)]), [bA[2 + s_]], [B_pT])
                cp("act", V(hT, i * 128, [(S, 8), (1, 128)]), V(pT, 0, [(128, 8), (1, 128)]), [B_pT], [B_hT[i]])

            if DBG and (DBGM & 1) and sq == 0 and l == int(os.environ.get("MK_DBGL", "0")):
                dma("sp", d_hT[:, :], V(hT, 0, [(1, 8 * S)]), B_hT, [])
            cv = Carver()
            R_STG, R_SQ, R_XW, R_T1, R_T2, R_QR, R_SS = 3, 3, 2, 4, 3, 2, 4
            stg = [cv.take(384, F32) for _ in range(R_STG)]
            sqr = [cv.take(320, F32) for _ in range(R_SQ)]
            xw = [cv.take(320, F32) for _ in range(R_XW)]
            t1 = [cv.take(320, F32) for _ in range(R_T1)]
            t2 = [cv.take(320, F32) for _ in range(R_T2)]
            qr = [cv.take(512, BF16) for _ in range(R_QR)]
            QKT = cv.take(4 * S, BF16)
            sag = cv.take(2 * S, BF16)
            PT = [cv.take(512, BF16) for _ in range(4)]
            Tt = [cv.take(512, F32) for _ in range(2)]
            bStg = [Buf("stg%d" % i) for i in range(R_STG)]
            bSq = [Buf("sq%d" % i) for i in range(R_SQ)]
            bSs = [Buf("ss%d" % i) for i in range(R_SS)]
            bXw = [Buf("xw%d" % i) for i in range(R_XW)]
            bT1 = [Buf("t1_%d" % i) for i in range(R_T1)]
            bT2 = [Buf("t2_%d" % i) for i in range(R_T2)]
            bQr = [Buf("qr%d" % i) for i in range(R_QR)]
            SSC = (8, 16, 24, 40)
            bQKT = [Buf("QKT%d" % i) for i in range(NT)]
            bSag = [[Buf("sag") for tb in range(4)] for b in range(2)]
            bPT = [Buf("PT%d" % i) for i in range(4)]
            bT = [Buf("T0"), Buf("T1")]
            bB = bStg + bSq + bSs + bXw + bT1 + bT2 + bQr + bQKT + bSag[0] + bSag[1] + bPT + bT
            inherit(bB, arena_bufs_prev)
            arena_bufs_prev = bB
            for s_ in range(R_QR):
                sc.op("pool", (lambda r_: (lambda e: e.memset(r_.ap(320, [(1, 128)]), 0.0)))(qr[s_]), [], [bQr[s_]])
            for g in range(4):
                sl, bw = load_w(l, 5 * g, 5)
                for b in range(2):
                    for tb in range(4):
                        pi = next_pb()
                        for kt in range(8):
                            mm(pbank[pi][:, :], wblk(sl, 3 + b, kt), hTs(kt, tb * 512, 512), kt == 0, kt == 7,
                               [bw] + B_hT[tb * 4:tb * 4 + 4], [B_pb[pi]])
                        act(sag.ap(b * S + tb * 512, [(1, 512)]), pbank[pi][:, :], AF.Silu, [B_pb[pi]], [bSag[b][tb]])
                def st1(i):
                    pi = next_pb()
                    for kt in range(8):
                        mm(pbank[pi][:, 0:384], hTs(kt, i * 128, 128), wrow(sl, 0, 3, kt), kt == 0, kt == 7,
                           [bw, B_hT[i]], [B_pb[pi]])
                    cp("act", stg[i % R_STG].ap(), pbank[pi][:, 0:384], [B_pb[pi]], [bStg[i % R_STG]])
                    act(sqr[i % R_SQ].ap(), pbank[pi][:, 0:320], AF.Square, [B_pb[pi]], [bSq[i % R_SQ]])

                def st2(i):
                    c0 = SSC[i % R_SS]
                    ssq = small[:, c0:c0 + 5]
                    sc.op("dve", (lambda o_, i_: (lambda e: e.tensor_reduce(out=o_, in_=i_, axis=AX.X, op=ALU.add)))(
                        ssq, sqr[i % R_SQ].ap(0, [(64, 5), (1, 64)])), [bSq[i % R_SQ]], [bSs[i % R_SS]])
                    act(ssq, ssq, AF.Sqrt, [bSs[i % R_SS], B_const], [bSs[i % R_SS]], scale=1.0 / 64, bias=epsb[:, :], ss=True)
                    tt("dve", xw[i % R_XW].ap(), stg[i % R_STG].ap(0, [(1, 320)]), wqkb[:, l, :], ALU.mult,
                       [bStg[i % R_STG], B_const], [bXw[i % R_XW]])
                    tt("pool", t1[i % R_T1].ap(0, [(64, 5), (1, 64)]), xw[i % R_XW].ap(0, [(64, 5), (1, 64)]),
                       V(ropeC, i * 64, [(0, 5), (1, 64)]), ALU.mult, [bXw[i % R_XW], B_const], [bT1[i % R_T1]])
                    for a in range(2):
                        tt("dve", t2[i % R_T2].ap(a * 32, [(64, 5), (16, 2), (1, 16)]),
                           xw[i % R_XW].ap(a * 32 + 16, [(64, 5), (-16, 2), (1, 16)]),
                           V(ropeS, i * 64 + a * 32, [(0, 5), (16, 2), (1, 16)]), ALU.mult, [bXw[i % R_XW], B_const], [bT2[i % R_T2]])
                    cp("pool", V(Va, i * 256, [(1, 64)]), stg[i % R_STG].ap(320, [(1, 64)]), [bStg[i % R_STG]], [B_Va[i]])
                    cp("pool", V(Va, i * 256 + 128 + 64, [(1, 64)]), stg[i % R_STG].ap(320, [(1, 64)]), [bStg[i % R_STG]], [B_Va[i]])

                def st3(i):
                    c0 = SSC[i % R_SS]
                    ssq = small[:, c0:c0 + 5]
                    recip(ssq, ssq, [bSs[i % R_SS]], [bSs[i % R_SS]], ss=True)
                    tt("pool", t1[i % R_T1].ap(), t1[i % R_T1].ap(), t2[i % R_T2].ap(), ALU.add, [bT1[i % R_T1], bT2[i % R_T2]], [bT1[i % R_T1]])

                def st4(i):
                    c0 = SSC[i % R_SS]
                    q_ = i % R_QR
                    tt("dve", qr[q_].ap(0, [(64, 5), (1, 64)]), t1[i % R_T1].ap(0, [(64, 5), (1, 64)]),
                       V(small, c0, [(1, 5), (0, 64)]), ALU.mult, [bT1[i % R_T1], bSs[i % R_SS]], [bQr[q_]], ss=True)
                    cp("pool", qr[q_].ap(448, [(1, 64)]), qr[q_].ap(256, [(1, 64)]), [bQr[q_]], [bQr[q_]])
                    for h in range(4):
                        tr(pT[:, h * 128:(h + 1) * 128], qr[q_].ap(h * 128, [(1, 128)]), [bQr[q_]], [B_pT])
                    cp("act", QKT.ap(i * 128, [(S, 4), (1, 128)]), V(pT, 0, [(128, 4), (1, 128)]), [B_pT], [bQKT[i]])

                for k in range(-3, NT):
                    for off, fn in ((3, st1), (2, st2), (1, st3), (0, st4)):
                        t_ = k + off
                        if 0 <= t_ < NT:
                            fn(t_)
                steps = [(h, tb, st) for h in range(4) for tb in range(4) for st in range(NT)]
                LA = 3
                PSS = (2, 3, 0, 1)

                def emit_qk(n):
                    h, tb, st = steps[n]
                    pq = PSS[n % 4]
                    mm(pbank[pq][:, :], QKT.ap((2 + h % 2) * S + st * 128, [(1, 128)]),
                       QKT.ap((h // 2) * S + tb * 512, [(1, 512)]), True, True,
                       [bQKT[st]] + bQKT[tb * 4:tb * 4 + 4], [B_pb[pq]])

                def emit_epi_pe(h, tb, tsl):
                    par = h % 2
                    b = h // 2
                    ft = 2 * g + b
                    orow = par * 64
                    mm(pbank[6][:, :], selsw, Tt[tsl].ap(), True, True, [bT[tsl], B_const], [B_pb[6]])
                    tt("dve", V(yaT, ft * S + tb * 512, [(1, 512)], orow, 64), Tt[tsl].ap(0, None, orow, 64),
                       pbank[6][orow:orow + 64, :], ALU.mult, [bT[tsl], B_pb[6]], [B_ya[ft][tb]])

                pend = []
                for n in range(LA):
                    emit_qk(n)
                for n, (h, tb, st) in enumerate(steps):
                    par = h % 2
                    b = h // 2
                    orow = par * 64
                    srow = 64 - orow
                    blk = n // NT
                    po = 4 + (blk % 2)
                    pq = PSS[n % 4]
                    pt = n % 4
                    act(PT[pt].ap(), pbank[pq][:, :], AF.Exp, [B_pb[pq]], [bPT[pt]], scale=0.125)
                    mm(pbank[po][:, :], V(Va, st * 256 + par * 128, [(1, 128)]), PT[pt].ap(), st == 0, st == NT - 1,
                       [bPT[pt], B_Va[st]], [B_pb[po]])
                    if n + LA < len(steps):
                        emit_qk(n + LA)
                    if st == NT - 1:
                        tsl = blk % 2
                        tt("dve", Tt[tsl].ap(0, None, orow, 64), pbank[po][orow:orow + 64, :],
                           sag.ap(b * S + tb * 512, [(1, 512)], orow, 64), ALU.mult, [B_pb[po], bSag[b][tb]], [bT[tsl]])
                        recip(Tt[tsl].ap(0, None, srow, 64), pbank[po][srow:srow + 64, :], [B_pb[po]], [bT[tsl]])
                        pend.append((n + 10, h, tb, tsl))
                    while pend and pend[0][0] <= n:
                        _, h_, tb_i, tsl_ = pend.pop(0)
                        emit_epi_pe(h_, tb_i, tsl_)
                for _, h_, tb_i, tsl_ in pend:
                    emit_epi_pe(h_, tb_i, tsl_)

            if DBG and (DBGM & 2) and sq == 0 and l == int(os.environ.get("MK_DBGL", "0")):
                dma("sp", d_ya[:, :], V(yaT, 0, [(1, 8 * S)]), [b for r_ in B_ya for b in r_], [])
            for ft in range(8):
                dma("sp", ya_d[ft], V(yaT, ft * S, [(1, S)]), B_ya[ft], [B_yad[ft]])
            cv = Carver()
            cv2 = Carver(1)
            t0 = cv.take(S, F32)
            tk = cv.take(S, BF16)
            tb_ = cv.take(S, F32)
            oT = cv.take(S, F32)
            QS = [cv.take(S, BF16), cv2.take(S, BF16)]
            VH = [cv.take(S, BF16), cv2.take(S, BF16)]
            SETS = []
            for c_ in (cv, cv2):
                SETS.append({"Qt": c_.take(S, BF16), "Kt": c_.take(S, BF16), "KhT": c_.take(S, BF16), "dcol": cv.take(64, F32),
                             "bQt": Buf("Qt"), "bKt": Buf("Kt"), "bKh": Buf("KhT"), "bdc": Buf("dcol")})
            HGS = [cv2.take(S, BF16), cv2.take(S, BF16)]
            nsq = [cv2.take(512, F32)] * 2
            nrs = [cv2.take(512, F32)] * 2
            Khm = [cv.take(4 * 128, BF16) for _ in range(2)]
            Zk = [cv.take(4 * 128, BF16) for _ in range(2)]
            Sf = [cv.take(4 * 128, F32) for _ in range(2)]
            Sb = [cv.take(4 * 128, BF16) for _ in range(4)]
            At = [cv.take(128, BF16) for _ in range(3)]
            zero_b = cv.take(128, BF16)
            bt0 = [Buf("t0_%d" % i) for i in range(4)]
            btk = Buf("tk")
            btb = Buf("tb")
            bqs = [[Buf("qs%d" % i) for i in range(4)] for _ in range(2)]
            boT = [Buf("oT%d" % i) for i in range(NT)]
            bhg = [[Buf("hg%d" % i) for i in range(4)] for _ in range(2)]
            bVh = [[Buf("Vh%d" % i) for i in range(4)] for _ in range(2)]
            bKhm = [Buf("Khm%d" % i) for i in range(2)]
            bZk = [Buf("Zk0"), Buf("Zk1")]
            bSf = [Buf("Sf0"), Buf("Sf1")]
            bSb = [Buf("Sb%d" % i) for i in range(4)]
            bAt = [Buf("At0"), Buf("At1"), Buf("At2")]
            bz = Buf("zero")
            _b1, _b2 = Buf("nsq"), Buf("nrs")
            bns = [_b1, _b1, _b2, _b2]
            bC1 = (bt0 + [btk, btb] + bqs[0] + boT + bVh[0] + [SETS[0][k] for k in ("bQt", "bKt", "bKh", "bdc")] + [SETS[1]["bdc"]]
                   + bKhm + bZk + bSf + bSb + bAt + [bz])
            bC2 = bqs[1] + bVh[1] + [SETS[1][k] for k in ("bQt", "bKt", "bKh")] + bhg[0] + bhg[1] + bns
            inherit(bC1, arena_bufs_prev)
            inherit(bC2, B_yad)
            arena_bufs_prev = bC1
            sc.op("pool", lambda e: e.memset(zero_b.ap(), 0.0), [], [bz])
            for z_ in range(2):
                sc.op("pool", (lambda r_: (lambda e: e.memset(r_.ap(), 0.0)))(Zk[z_]), [], [bZk[z_]])
            headw = {}

            def proj_units(j, blk, tb, evac):
                stp = {}
                u = []
                for q_ in range(4):
                    def f(q_=q_):
                        sl, bw = headw[j]
                        if q_ == 0:
                            stp["pi"] = next_pb()
                        pi = stp["pi"]
                        for kt in (2 * q_, 2 * q_ + 1):
                            mm(pbank[pi][:, :], wblk(sl, blk, kt), hTs(kt, tb * 512, 512), kt == 0, kt == 7,
                               [bw] + B_hT[tb * 4:tb * 4 + 4], [B_pb[pi]])
                        if q_ == 3:
                            evac(pi)
                    u.append(f)
                return u

            def P_units(j):
                par = j % 2
                u = []

                def ld():
                    headw[j] = load_w(l, 20 + 5 * j, 5)
                u.append(ld)
                for tb in range(4):
                    u += proj_units(j, 0, tb, lambda pi, tb=tb: act(QS[par].ap(tb * 512, [(1, 512)]), pbank[pi][:, :], AF.Silu,
                                                                     [B_pb[pi]], [bqs[par][tb]]))
                    stv = {}
                    for ii in range(4):
                        for hq_ in range(2):
                            def f_v(tb=tb, ii=ii, hq_=hq_, stv=stv):
                                sl, bw = headw[j]
                                if ii == 0 and hq_ == 0:
                                    stv["pi"] = next_pb()
                                pi = stv["pi"]
                                i = tb * 4 + ii
                                for kt in range(4 * hq_, 4 * hq_ + 4):
                                    mm(pbank[pi][:, ii * 128:(ii + 1) * 128], hTs(kt, i * 128, 128), wblk(sl, 2, kt), kt == 0, kt == 7,
                                       [bw, B_hT[i]], [B_pb[pi]])
                                if ii == 3 and hq_ == 1:
                                    cp("act", VH[par].ap(tb * 512, [(1, 512)]), pbank[pi][:, :], [B_pb[pi]], [bVh[par][tb]])
                            u.append(f_v)
                    u += proj_units(j, 1, tb, lambda pi, tb=tb: act(HGS[par].ap(tb * 512, [(1, 512)]), pbank[pi][:, :], AF.Silu,
                                                                     [B_pb[pi]], [bhg[par][tb]]))
                return u

            def prep_units(j, dr):
                par = j % 2
                X = SETS[dr]
                lbi = l * 16 + dr * 8 + j
                u = []
                for tb in range(4):
                    u += proj_units(j, 3 + dr, tb, lambda pi, tb=tb: act(t0.ap(tb * 512, [(1, 512)]), pbank[pi][:, :], AF.Sigmoid,
                                                                          [B_pb[pi]], [bt0[tb]]))
                u.append(lambda: ts("dve", t0.ap(), t0.ap(), omlt[:, lbi:lbi + 1], lbt[:, lbi:lbi + 1], ALU.mult, ALU.add, bt0 + [B_const], bt0))
                u.append(lambda: ts("pool", tk.ap(), t0.ap(), -1.0, 1.0, ALU.mult, ALU.add, bt0, [btk]))
                u.append(lambda: act(t0.ap(), t0.ap(), AF.Ln, bt0, bt0))
                if dr == 0:
                    u.append(lambda: sc.op("dve", lambda e: e.tensor_tensor_scan(out=tb_.ap(), data0=smask[:, :], data1=t0.ap(), initial=0.0,
                                                                                 op0=ALU.mult, op1=ALU.add), bt0 + [B_const], [btb]))
                else:
                    u.append(lambda: sc.op("dve", lambda e: e.tensor_tensor_scan(out=tb_.ap(S - 1, [(-1, S)]), data0=smask[:, :],
                                                                                 data1=t0.ap(S - 1, [(-1, S)]), initial=0.0,
                                                                                 op0=ALU.mult, op1=ALU.add), bt0 + [B_const], [btb]))
                u.append(lambda: act(t0.ap(), tb_.ap(), AF.Exp, [btb], bt0))
                u.append(lambda: stt(X["Qt"].ap(), QS[par].ap(), 128.0 ** -0.5, t0.ap(), ALU.mult, ALU.mult, bqs[par] + bt0, [X["bQt"]]))
                dpos = 31 if dr == 0 else 0
                u.append(lambda: act(X["dcol"].ap(), tb_.ap(dpos, [(32, 64)]), AF.Exp, [btb], [X["bdc"]]))
                u.append(lambda: act(t0.ap(), tb_.ap(), AF.Exp, [btb], bt0, scale=-1.0))
                u.append(lambda: tt("dve", t0.ap(), tk.ap(), t0.ap(), ALU.mult, [btk] + bt0, bt0))
                u.append(lambda: cp("act", X["Kt"].ap(), t0.ap(), bt0, [X["bKt"]]))
                u.append(lambda: tt("dve", X["KhT"].ap(0, [(32, 64), (1, 32)]), t0.ap(0, [(32, 64), (1, 32)]),
                                    X["dcol"].ap(0, [(1, 64), (0, 32)]), ALU.mult, bt0 + [X["bdc"]], [X["bKh"]]))
                return u

            def chain_units(j, dr):
                par = j % 2
                X = SETS[dr]
                Qt, Kt, KhT, dcol = X["Qt"], X["Kt"], X["KhT"], X["dcol"]
                bQt, bKt, bKh, bdc = X["bQt"], X["bKt"], X["bKh"], X["bdc"]
                Vh = VH[par]
                bV = bVh[par]
                tiles = list(range(NT)) if dr == 0 else list(range(NT - 1, -1, -1))
                mk = maskF if dr == 0 else maskB
                order = list(range(4)) if dr == 0 else list(range(3, -1, -1))
                st_ = {"cur": None}
                enter = {}

                def front(idx, i):
                    ib = i // 4
                    zs = idx % 2
                    for jj in range(4):
                        cp("pool", Zk[zs].ap(jj * 128 + jj * 32, [(1, 32)]), KhT.ap(i * 128 + jj * 32, [(1, 32)]), [bKh], [bZk[zs]])
                    for jj in range(4):
                        tr(pT[:, jj * 128:(jj + 1) * 128], Zk[zs].ap(jj * 128, [(1, 128)]), [bZk[zs]], [B_pT])
                    cp("act", Khm[zs].ap(), pT[:, 0:512], [B_pT], [bKhm[zs]])
                    mm(pbank[2][:, 0:128], Kt.ap(i * 128, [(1, 128)]), Qt.ap(i * 128, [(1, 128)]), True, True, [bKt, bQt], [B_pb[2]])
                    a_ = idx % 3
                    tt("dve", At[a_].ap(), pbank[2][:, 0:128], mk, ALU.mult, [B_pb[2], B_const], [bAt[a_]])
                    kb = (3, 6)[idx % 2]
                    for jj in order:
                        mm(pbank[kb][:, jj * 128:(jj + 1) * 128], Khm[zs].ap(jj * 128, [(1, 128)]), Vh.ap(i * 128, [(1, 128)]), True, True,
                           [bKhm[zs], bV[ib]], [B_pb[kb]])
                    hs = idx % 2
                    sbs = idx % 4
                    for n_, jj in enumerate(order):
                        c = i * 4 + jj
                        enter[c] = st_["cur"]
                        dst = Sf[hs].ap(n_ * 128, [(1, 128)])
                        if st_["cur"] is None:
                            cp("dve", dst, pbank[kb][:, jj * 128:(jj + 1) * 128], [B_pb[kb]], [bSf[hs]])
                        else:
                            ph, pn = st_["cur"]
                            stt(dst, Sf[ph % 2].ap(pn * 128, [(1, 128)]), dcol.ap(c, [(1, 1)]), pbank[kb][:, jj * 128:(jj + 1) * 128],
                                ALU.mult, ALU.add, [bSf[ph % 2], bdc, B_pb[kb]], [bSf[hs]])
                        st_["cur"] = (idx, n_)
                    cp("dve", Sb[sbs].ap(), Sf[hs].ap(), [bSf[hs]], [bSb[sbs]])

                def back(idx, i):
                    ib = i // 4
                    a_ = idx % 3
                    po = 4 + (idx % 2)
                    mm(pbank[po][:, 0:128], Vh.ap(i * 128, [(1, 128)]), At[a_].ap(), True, False, [bV[ib], bAt[a_]], [B_pb[po]])
                    for n_, jj in enumerate(order):
                        c = i * 4 + jj
                        es_ = enter[c]
                        stat = zero_b.ap() if es_ is None else Sb[es_[0] % 4].ap(es_[1] * 128, [(1, 128)])
                        rb = [bz] if es_ is None else [bSb[es_[0] % 4]]
                        mm(pbank[po][:, jj * 32:(jj + 1) * 32], stat, Qt.ap(c * 32, [(1, 32)]), False, n_ == 3,
                           rb + [bQt], [B_pb[po]])
                    if dr == 0:
                        cp("act", oT.ap(i * 128, [(1, 128)]), pbank[po][:, 0:128], [B_pb[po]], [boT[i]])
                    else:
                        tt("dve", oT.ap(i * 128, [(1, 128)]), pbank[po][:, 0:128], oT.ap(i * 128, [(1, 128)]), ALU.add,
                           [B_pb[po], boT[i]], [boT[i]])

                u = [lambda: front(0, tiles[0]), lambda: front(1, tiles[1])]
                for idx, i in enumerate(tiles):
                    if idx + 2 < NT:
                        u.append(lambda idx=idx: front(idx + 2, tiles[idx + 2]))
                    u.append(lambda idx=idx, i=i: back(idx, i))
                return u

            def norm_units(j):
                u = []
                for tb in range(4):
                    def f_n(tb=tb):
                        n_ = tb % 2
                        oblk = oT.ap(tb * 512, [(1, 512)])
                        tt("pool", nsq[n_].ap(), oblk, oblk, ALU.mult, boT[tb * 4:tb * 4 + 4], [bns[n_]])
                        mm(pbank[6][:, :], onesf, nsq[n_].ap(), True, True, [bns[n_], B_const], [B_pb[6]])
                        act(nrs[n_].ap(), pbank[6][:, :], AF.Ln, [B_pb[6], B_const], [bns[2 + n_]], scale=1.0 / 128, bias=epsb[:, :])
                        act(nrs[n_].ap(), nrs[n_].ap(), AF.Exp, [bns[2 + n_]], [bns[2 + n_]], scale=-0.5)
                        tt("dve", nrs[n_].ap(), nrs[n_].ap(), oblk, ALU.mult, [bns[2 + n_]] + boT[tb * 4:tb * 4 + 4], [bns[2 + n_]])
                        stt(V(yhT, j * S + tb * 512, [(1, 512)]), nrs[n_].ap(), gw[:, l:l + 1], HGS[j % 2].ap(tb * 512, [(1, 512)]),
                            ALU.mult, ALU.mult, [bns[2 + n_], bhg[j % 2][tb], B_const], [B_yh[j][tb]])
                    u.append(f_n)
                return u

            def merge(primary, secondary):
                np_, ns_ = len(primary), len(secondary)
                si = 0
                for pi_, pu in enumerate(primary):
                    pu()
                    target = (pi_ + 1) * ns_ // np_
                    while si < target:
                        secondary[si]()
                        si += 1
                while si < ns_:
                    secondary[si]()
                    si += 1

            for u_ in P_units(0) + prep_units(0, 0):
                u_()
            for j in range(8):
                merge(chain_units(j, 0), prep_units(j, 1))
                sec = (P_units(j + 1) + prep_units(j + 1, 0)) if j < 7 else []
                merge(chain_units(j, 1) + norm_units(j), sec)
            inherit([b for r_ in B_ya for b in r_], bC2)
            for ft in range(8):
                dma("sp", V(yaT, ft * S, [(1, S)]), ya_d[ft], [B_yad[ft]], B_ya[ft])

            if DBG and (DBGM & 4) and sq == 0 and l == int(os.environ.get("MK_DBGL", "0")):
                dma("sp", d_yh[:, :], V(yhT, 0, [(1, 8 * S)]), [b for r_ in B_yh for b in r_], [])
            cv = Carver()
            wo = cv.take(8 * 1024, BF16)
            mT = cv.take(8 * 512, BF16)
            sa = [cv.take(512, F32) for _ in range(2)]
            sh = [cv.take(512, F32) for _ in range(2)]
            xr = [cv.take(D, F32) for _ in range(2)]
            xo = [cv.take(D, F32) for _ in range(2)]
            junkD = cv.take(D, BF16)
            bwo = Buf("wo")
            bmT = [Buf("mT%d" % i) for i in range(8)]
            bsa = [Buf("sa0"), Buf("sa1")]
            bsh = [Buf("sh0"), Buf("sh1")]
            bxr = [Buf("xr0"), Buf("xr1")]
            bxo = [Buf("xo0"), Buf("xo1")]
            bjk = Buf("junkD")
            bsm = Buf("smD")
            bD = [bwo] + bmT + bsa + bsh + bxr + bxo + [bjk, bsm]
            inherit(bD, arena_bufs_prev)
            arena_bufs_prev = bD
            if last:
                dma("sp", nwb[:, :], AP(fnw_d, 0, [[0, 128], [1, D]]), [], [B_nwb])
            dma("sp", wo.ap(0, [(1024, 8), (1, 1024)]), wsc_d[l, 92:100].rearrange("e p c -> p e c"),
                [B_wsc[l][p] for p in range(46, 50)], [bwo])
            for tb in range(4):
                for db in range(8):
                    sl, bw = load_w(l, 60 + 4 * db, 4)
                    s_ = db % 2
                    for which, dst, bdst in ((0, sa, bsa), (1, sh, bsh)):
                        pi = next_pb()
                        for kt in range(8):
                            mm(pbank[pi][:, :], wblk(sl, which, kt), hTs(kt, tb * 512, 512), kt == 0, kt == 7,
                               [bw] + B_hT[tb * 4:tb * 4 + 4], [B_pb[pi]])
                        act(dst[s_].ap(), pbank[pi][:, :], AF.Sigmoid, [B_pb[pi]], [bdst[s_]])
                    pi = next_pb()
                    for kt in range(8):
                        mm(pbank[pi][:, :], wblk(sl, 2, kt), V(yaT, kt * S + tb * 512, [(1, 512)]), kt == 0, kt == 7,
                           [bw, B_ya[kt][tb]], [B_pb[pi]])
                    tt("dve", sa[s_].ap(), pbank[pi][:, :], sa[s_].ap(), ALU.mult, [B_pb[pi], bsa[s_]], [bsa[s_]])
                    pi = next_pb()
                    for kt in range(8):
                        mm(pbank[pi][:, :], wblk(sl, 3, kt), V(yhT, kt * S + tb * 512, [(1, 512)]), kt == 0, kt == 7,
                           [bw, B_yh[kt][tb]], [B_pb[pi]])
                    tt("dve", sh[s_].ap(), pbank[pi][:, :], sh[s_].ap(), ALU.mult, [B_pb[pi], bsh[s_]], [bsh[s_]])
                    tt("pool", mT.ap(db * 512, [(1, 512)]), sa[s_].ap(), sh[s_].ap(), ALU.add, [bsa[s_], bsh[s_]], [bmT[db]])
                for ii in range(4):
                    i = tb * 4 + ii
                    s_ = i % 2
                    r0 = tok0 + i * 128
                    dma("sp", xr[s_].ap(), src_d[r0:r0 + 128, :], [B_x1[i]], [bxr[s_]])
                    for hf in range(2):
                        pi = next_pb()
                        for kt in range(8):
                            mm(pbank[pi][:, :], mT.ap(kt * 512 + ii * 128, [(1, 128)]),
                               wo.ap(hf * 4 * 1024 + kt * 128, [(1024, 4), (1, 128)]), kt == 0, kt == 7,
                               [bmT[kt], bwo], [B_pb[pi]])
                        tt("dve", xo[s_].ap(hf * 512, [(1, 512)]), pbank[pi][:, :], xr[s_].ap(hf * 512, [(1, 512)]), ALU.add,
                           [B_pb[pi], bxr[s_]], [bxo[s_]])
                    if DBG and (DBGM & 8) and sq == 0 and l == int(os.environ.get("MK_DBGL", "0")):
                        dma("sp", d_x1[i * 128:(i + 1) * 128, :], xo[s_].ap(), [bxo[s_]], [])
                    if not last:
                        dma("sp", x1_d[r0:r0 + 128, :], xo[s_].ap(), [bxo[s_]], [B_x1[i]])
                    else:
                        ssq = small[:, 32 + s_:33 + s_]
                        act(junkD.ap(), xo[s_].ap(), AF.Square, [bxo[s_]], [bjk, bsm], accum_out=ssq)
                        act(ssq, ssq, AF.Sqrt, [bsm, B_const], [bsm], scale=1.0 / D, bias=epsb[:, :], ss=True)
                        recip(ssq, ssq, [bsm], [bsm], ss=True)
                        stt(xo[s_].ap(), xo[s_].ap(), ssq, nwb[:, :], ALU.mult, ALU.mult, [bxo[s_], bsm, B_nwb], [bxo[s_]], ss=True)
                        dma("sp", out_d[r0:r0 + 128, :], xo[s_].ap(), [bxo[s_]], [])

    sc.finalize()
    sems = {}
    for s_ in Sched.STREAMS:
        n = (sc.nsig.get(s_, 0) + CH - 1) // CH
        sems[s_] = [es.enter_context(nc.semaphore("c_%s_%d" % (s_, i))) for i in range(max(n, 1))]
    for dom, n in sc.ndma.items():
        sems[dom] = [es.enter_context(nc.semaphore("%s_%d" % (dom, i))) for i in range(min(KD, n))]
    with nc.Block() as block:
        @block.tensor
        def _(e):
            sc.emit("pe", e, sems)

        @block.scalar
        def _(e):
            sc.emit("act", e, sems)

        @block.vector
        def _(e):
            sc.emit("dve", e, sems)

        @block.gpsimd
        def _(e):
            sc.emit("pool", e, sems)

        @block.sync
        def _(e):
            sc.emit("sp", e, sems, final_wait=True)
    es.close()
    return nc


def _perm_cols():
    cols = []
    for g in range(4):
        cols += list(range(O_AQ + 256 * g, O_AQ + 256 * g + 256))
        cols += list(range(O_AK + 64 * g, O_AK + 64 * g + 64))
        cols += list(range(O_AV + 64 * g, O_AV + 64 * g + 64))
        cols += list(range(O_AG + 256 * g, O_AG + 256 * g + 256))
    for j in range(8):
        for base in (O_HQ, O_HG, O_HI, O_HFF, O_HFB):
            cols += list(range(base + 128 * j, base + 128 * j + 128))
    return np.array(cols, dtype=np.int64)


def _blocks(w):
    n = w.shape[1] // 128
    return np.ascontiguousarray(w.reshape(8, 128, n, 128).transpose(2, 1, 0, 3)).reshape(n, 128, 1024)


def _host_consts():
    cm = np.zeros((6, 128, 128), np.float32)
    cm[0] = np.eye(128, dtype=np.float32)
    s_ = np.arange(128)[:, None]
    t_ = np.arange(128)[None, :]
    same = (s_ // 32) == (t_ // 32)
    cm[1] = (same & (s_ <= t_)).astype(np.float32)
    cm[2] = (same & (s_ >= t_)).astype(np.float32)
    cm[3] = (t_ == (s_ + 64) % 128).astype(np.float32)
    cm[4] = 1.0
    for jj in range(4):
        cm[5][32 * jj:32 * jj + 32, jj] = 1.0
    smask = np.ones((128, S), np.float32)
    smask[:, ::32] = 0.0
    pos = np.arange(S)
    row = (pos // 64).astype(np.float32)
    col = (pos % 64).astype(np.float32)
    inv = (10000.0 ** (-np.arange(0, 32, 2, dtype=np.float32) / 32)).astype(np.float32)
    ar = row[:, None] * inv
    ac = col[:, None] * inv
    C = np.concatenate([np.cos(ar), np.cos(ar), np.cos(ac), np.cos(ac)], axis=1).astype(np.float32)
    Sn = np.concatenate([-np.sin(ar), np.sin(ar), -np.sin(ac), np.sin(ac)], axis=1).astype(np.float32)
    rope = np.stack([C, Sn]).astype(np.float32)
    return cm, smask, rope


_NC_CACHE = {}


def kernel(x, w_in, norm_w, q_norm_w, k_norm_w, hgrn_lower_bounds, hgrn_norm_w,
           w_branch_attn, w_branch_hgrn, w_out, final_norm_w):
    ncores = 8
    nseq = int(os.environ.get("MK_NSEQ", "4"))
    nl = int(os.environ.get("MK_NL", "2"))
    x = np.asarray(x, np.float32)
    w_in = np.asarray(w_in, np.float32)
    perm = _perm_cols()
    wall = np.zeros((2, NBLK, 128, 1024), np.float32)
    for l in range(2):
        wall[l, 0:60] = _blocks(w_in[l][:, perm])
        ma = _blocks(w_in[l][:, O_MA:O_MA + 1024])
        mh = _blocks(w_in[l][:, O_MH:O_MH + 1024])
        wa = _blocks(np.asarray(w_branch_attn[l], np.float32))
        wh = _blocks(np.asarray(w_branch_hgrn[l], np.float32))
        for db in range(8):
            wall[l, 60 + 4 * db + 0] = ma[db]
            wall[l, 60 + 4 * db + 1] = mh[db]
            wall[l, 60 + 4 * db + 2] = wa[db]
            wall[l, 60 + 4 * db + 3] = wh[db]
        wall[l, 92:100] = _blocks(np.asarray(w_out[l], np.float32))
    cm, smask, rope = _host_consts()
    qw = np.asarray(q_norm_w, np.float32)
    kw = np.asarray(k_norm_w, np.float32)
    qkw = np.concatenate([np.tile(qw, (1, 4)), kw], axis=1).astype(np.float32)
    lb = np.asarray(hgrn_lower_bounds, np.float32)
    lbraw = np.ascontiguousarray(lb.reshape(2, 2, 8, 128).transpose(3, 1, 0, 2)).reshape(128, 32)
    gwh = np.ascontiguousarray(np.asarray(hgrn_norm_w, np.float32).T)
    nw = np.asarray(norm_w, np.float32)
    fnw = np.asarray(final_norm_w, np.float32).reshape(1, D)

    key = (nseq, nl)
    if key not in _NC_CACHE:
        _NC_CACHE[key] = build(nseq, nl)
    nc = _NC_CACHE[key]
    B = x.shape[0]
    per = B // ncores
    in_maps = []
    for c in range(ncores):
        xs = x[c * per: c * per + nseq].reshape(nseq * S, D)
        in_maps.append({"x": np.ascontiguousarray(xs), "wall": wall, "cmat": cm, "smask": smask, "rope": rope,
                        "nw": nw, "fnw": fnw, "qkw": qkw, "lbraw": lbraw, "gw": gwh})
    res = run_bass_kernel_spmd(nc, in_maps, core_ids=list(range(ncores)))
    if os.environ.get("MK_DBG", "0") == "1":
        kernel.dbg = {k: np.asarray(res.results[0][k]) for k in ("d_hT", "d_ya", "d_yh", "d_x1")}
    out = np.zeros((B, S, D), np.float32)
    for c in range(ncores):
        out[c * per: c * per + nseq] = np.asarray(res.results[c]["out"], np.float32).reshape(nseq, S, D)
    return out
```
